# Optimizing a Trainium2 kernel written in Bass

```python
import math
import jax
import jax.numpy as jnp
from jax import lax
import numpy as np

D_MODEL = 1024
BATCH = 8
SEQ = 4096
DEPTH = 4

HEAD_DIM = 64
A_HEADS = D_MODEL // 4 // HEAD_DIM
A_QK_DIM = HEAD_DIM // 2
B_HEADS = D_MODEL // 2 // HEAD_DIM
C_HEADS = D_MODEL // 4 // HEAD_DIM
C_PAIRS = ((128, 1), (512, 4), (2048, 16))
C_GROUPS = len(C_PAIRS)
A_W = A_HEADS * HEAD_DIM
B_W = B_HEADS * HEAD_DIM
C_W = C_HEADS * HEAD_DIM
MIX_W = A_W + B_W + C_W
A_OFF = 0
B_OFF = A_OFF + 3 * A_W
F_OFF = B_OFF + 3 * B_W
C_OFF = F_OFF + B_HEADS
N_IN = C_OFF + 3 * C_GROUPS * C_W
D_FF = 4 * D_MODEL
N_BUCKETS = 32
MAX_DISTANCE = 2048
N_BIAS_HEADS = A_HEADS + C_GROUPS * C_HEADS
Q_BLOCK = 128
NORM_EPS = 1e-6
NEG_INF = -1e30

kernel_name = "hybrid_diff_fox_dilated_trunk"


def rmsnorm(x, g):
    xf = x.astype(jnp.float32)
    y = xf * lax.rsqrt(jnp.mean(xf * xf, axis=-1, keepdims=True) + NORM_EPS)
    return (y * g.astype(jnp.float32)).astype(x.dtype)


def t5_bucket(dist):
    max_exact = N_BUCKETS // 2
    d = jnp.maximum(dist, 0)
    df = jnp.maximum(d, 1).astype(jnp.float32)
    large = max_exact + (jnp.log(df / max_exact) / math.log(MAX_DISTANCE / max_exact)
                         * (N_BUCKETS - max_exact)).astype(jnp.int32)
    large = jnp.minimum(large, N_BUCKETS - 1)
    return jnp.where(d < max_exact, d, large)


def diff_attention(q, k, v, lam, lam_init, sub_g, bias_a):
    bsz, seq = q.shape[0], q.shape[1]
    scale = A_QK_DIM ** -0.5
    kpos = jnp.arange(seq)

    def block(i):
        q0 = i * Q_BLOCK
        qb = lax.dynamic_slice_in_dim(q, q0, Q_BLOCK, axis=1)
        dist = (q0 + jnp.arange(Q_BLOCK))[:, None] - kpos[None, :]
        bias = jnp.transpose(bias_a[t5_bucket(dist)], (2, 0, 1)).astype(jnp.float32)
        logits = jnp.einsum('bqhcd,bkhcd->bchqk', qb, k).astype(jnp.float32) * scale + bias
        logits = jnp.where(dist >= 0, logits, NEG_INF)
        p = jax.nn.softmax(logits, axis=-1)
        a = p[:, 0] - lam * p[:, 1]
        return jnp.einsum('bhqk,bkhd->bqhd', a.astype(v.dtype), v)

    out = lax.map(block, jnp.arange(seq // Q_BLOCK))
    out = jnp.moveaxis(out, 0, 1).reshape(bsz, seq, A_HEADS, HEAD_DIM)
    return rmsnorm(out, sub_g) * (1.0 - lam_init)


def forgetting_attention(q, k, v, f_logit):
    bsz, seq = q.shape[0], q.shape[1]
    scale = HEAD_DIM ** -0.5
    log_f = jax.nn.log_sigmoid(f_logit.astype(jnp.float32))
    cum = jnp.transpose(jnp.cumsum(log_f, axis=1), (0, 2, 1))
    kpos = jnp.arange(seq)

    def block(i):
        q0 = i * Q_BLOCK
        qb = lax.dynamic_slice_in_dim(q, q0, Q_BLOCK, axis=1)
        cq = lax.dynamic_slice_in_dim(cum, q0, Q_BLOCK, axis=2)
        causal = (q0 + jnp.arange(Q_BLOCK))[:, None] >= kpos[None, :]
        decay = cq[:, :, :, None] - cum[:, :, None, :]
        logits = jnp.einsum('bqhd,bkhd->bhqk', qb, k).astype(jnp.float32) * scale + decay
        logits = jnp.where(causal, logits, NEG_INF)
        p = jax.nn.softmax(logits, axis=-1)
        return jnp.einsum('bhqk,bkhd->bqhd', p.astype(v.dtype), v)

    out = lax.map(block, jnp.arange(seq // Q_BLOCK))
    return jnp.moveaxis(out, 0, 1).reshape(bsz, seq, B_HEADS, HEAD_DIM)


def dilated_attention(q, k, v, bias_c):
    bsz, seq = q.shape[0], q.shape[1]
    scale = HEAD_DIM ** -0.5
    qs = [q[:, :, g] for g in range(C_GROUPS)]
    ks = [k[:, :, g] for g in range(C_GROUPS)]
    vs = [v[:, :, g] for g in range(C_GROUPS)]

    def block(i):
        q0 = i * Q_BLOCK
        qpos = q0 + jnp.arange(Q_BLOCK)
        outs, lses = [], []
        for g, (window, dil) in enumerate(C_PAIRS):
            n_keys = window // dil + 1
            dist = jnp.arange(n_keys) * dil
            kidx = qpos[:, None] - dist[None, :]
            valid = kidx >= 0
            kidx = jnp.maximum(kidx, 0)
            kg = jnp.take(ks[g], kidx, axis=1)
            vg = jnp.take(vs[g], kidx, axis=1)
            qg = lax.dynamic_slice_in_dim(qs[g], q0, Q_BLOCK, axis=1)
            bias = bias_c[t5_bucket(dist), g * C_HEADS:(g + 1) * C_HEADS]
            logits = (jnp.einsum('bqhd,bqkhd->bhqk', qg, kg).astype(jnp.float32) * scale
                      + jnp.transpose(bias).astype(jnp.float32)[None, :, None, :])
            logits = jnp.where(valid[None, None], logits, NEG_INF)
            m = jnp.max(logits, axis=-1, keepdims=True)
            e = jnp.exp(logits - m)
            s = jnp.sum(e, axis=-1, keepdims=True)
            outs.append(jnp.einsum('bhqk,bqkhd->bqhd', (e / s).astype(vg.dtype), vg))
            lses.append((m + jnp.log(s))[..., 0])
        alpha = jax.nn.softmax(jnp.stack(lses, axis=0), axis=0)
        alpha = jnp.transpose(alpha, (0, 1, 3, 2))[..., None]
        return jnp.sum(alpha * jnp.stack(outs, axis=0).astype(jnp.float32), axis=0)

    out = lax.map(block, jnp.arange(seq // Q_BLOCK))
    return jnp.moveaxis(out, 0, 1).reshape(bsz, seq, C_HEADS, HEAD_DIM).astype(q.dtype)


def setup_inputs(seed: int = 0) -> dict:
    key = jax.random.key(seed)
    ks = jax.random.split(key, 16)
    L, D = DEPTH, D_MODEL

    def nrm(k, shape, s):
        return jax.random.normal(k, shape, jnp.float32) * s

    return {
        'x': nrm(ks[0], (BATCH, SEQ, D), 1.0),
        'norm1_g': 1.0 + nrm(ks[1], (L, D), 0.02),
        'w_in': nrm(ks[2], (L, D, N_IN), D ** -0.5),
        'b_f': 2.0 + nrm(ks[3], (L, B_HEADS), 0.1),
        'lam_q1': nrm(ks[4], (L, A_QK_DIM), 0.1),
        'lam_k1': nrm(ks[5], (L, A_QK_DIM), 0.1),
        'lam_q2': nrm(ks[6], (L, A_QK_DIM), 0.1),
        'lam_k2': nrm(ks[7], (L, A_QK_DIM), 0.1),
        'diff_norm_g': 1.0 + nrm(ks[8], (L, HEAD_DIM), 0.02),
        'w_o': nrm(ks[9], (L, MIX_W, D), MIX_W ** -0.5),
        'norm2_g': 1.0 + nrm(ks[10], (L, D), 0.02),
        'w_1': nrm(ks[11], (L, D, D_FF), D ** -0.5),
        'w_2': nrm(ks[12], (L, D_FF, D), D_FF ** -0.5),
        'rel_bias': nrm(ks[13], (N_BUCKETS, N_BIAS_HEADS), 0.5),
        'final_g': 1.0 + nrm(ks[14], (D,), 0.02),
    }


def reference(x, norm1_g, w_in, b_f, lam_q1, lam_k1, lam_q2, lam_k2, diff_norm_g,
              w_o, norm2_g, w_1, w_2, rel_bias, final_g):
    bsz, seq = x.shape[0], x.shape[1]
    bias_a = rel_bias[:, :A_HEADS]
    bias_c = rel_bias[:, A_HEADS:]
    for l in range(DEPTH):
        h = rmsnorm(x, norm1_g[l])
        proj = jnp.einsum('bsd,dn->bsn', h, w_in[l])

        qa = proj[..., A_OFF:A_OFF + A_W].reshape(bsz, seq, A_HEADS, 2, A_QK_DIM)
        ka = proj[..., A_OFF + A_W:A_OFF + 2 * A_W].reshape(bsz, seq, A_HEADS, 2, A_QK_DIM)
        va = proj[..., A_OFF + 2 * A_W:B_OFF].reshape(bsz, seq, A_HEADS, HEAD_DIM)
        lam_init = 0.8 - 0.6 * math.exp(-0.3 * l)
        lam = (jnp.exp(jnp.sum(lam_q1[l].astype(jnp.float32) * lam_k1[l].astype(jnp.float32)))
               - jnp.exp(jnp.sum(lam_q2[l].astype(jnp.float32) * lam_k2[l].astype(jnp.float32)))
               + lam_init)
        out_a = diff_attention(qa, ka, va, lam, lam_init, diff_norm_g[l], bias_a)

        qb = proj[..., B_OFF:B_OFF + B_W].reshape(bsz, seq, B_HEADS, HEAD_DIM)
        kb = proj[..., B_OFF + B_W:B_OFF + 2 * B_W].reshape(bsz, seq, B_HEADS, HEAD_DIM)
        vb = proj[..., B_OFF + 2 * B_W:F_OFF].reshape(bsz, seq, B_HEADS, HEAD_DIM)
        f_logit = proj[..., F_OFF:C_OFF] + b_f[l]
        out_b = forgetting_attention(qb, kb, vb, f_logit)

        qc = proj[..., C_OFF:C_OFF + C_GROUPS * C_W].reshape(bsz, seq, C_GROUPS, C_HEADS, HEAD_DIM)
        kc = proj[..., C_OFF + C_GROUPS * C_W:C_OFF + 2 * C_GROUPS * C_W].reshape(bsz, seq, C_GROUPS, C_HEADS, HEAD_DIM)
        vc = proj[..., C_OFF + 2 * C_GROUPS * C_W:N_IN].reshape(bsz, seq, C_GROUPS, C_HEADS, HEAD_DIM)
        out_c = dilated_attention(qc, kc, vc, bias_c)

        mixed = jnp.concatenate([out_a.reshape(bsz, seq, A_W).astype(x.dtype),
                                 out_b.reshape(bsz, seq, B_W).astype(x.dtype),
                                 out_c.reshape(bsz, seq, C_W).astype(x.dtype)], axis=-1)
        x = x + jnp.einsum('bsm,md->bsd', mixed, w_o[l])

        h2 = rmsnorm(x, norm2_g[l])
        u = jnp.square(jax.nn.relu(jnp.einsum('bsd,df->bsf', h2, w_1[l])))
        x = x + jnp.einsum('bsf,fd->bsd', u, w_2[l])
    return rmsnorm(x, final_g)
```

```python
import math
from contextlib import ExitStack
import numpy as np
import concourse.bass as bass
import concourse.mybir as mybir
from concourse.bass_utils import run_bass_kernel_spmd

F32 = mybir.dt.float32
BF16 = mybir.dt.bfloat16
AF = mybir.ActivationFunctionType
ALU = mybir.AluOpType

S = 4096
D = 1024
L = 4
NIN = 4616
DFF = 4096
EPS = 1e-6
A_OFF, B_OFF, F_OFF, C_OFF = 0, 768, 2304, 2312
DILS = (1, 4, 16)
SCALE_A = 32 ** -0.5
SCALE_B = 0.125
NEG = -30000.0
TA_LEN = 4352
DTHR = 1600
TC_LEN = 512


class Buf:
    __slots__ = ("w", "r")

    def __init__(self):
        self.w = None
        self.r = {}


class Sched:
    def __init__(self, nc, es):
        self.nc = nc
        self.eng = {"pe": nc.tensor, "act": nc.scalar, "dve": nc.vector,
                    "pool": nc.gpsimd, "sp": nc.sync, "cv": nc.gpsimd}
        self.sem = {}
        self.cnt = {}
        for e in ("pe", "act", "dve", "pool"):
            self.sem[e] = es.enter_context(nc.semaphore("s_" + e))
            self.cnt[e] = 0
        self.slots = {}
        for q, n in (("sp", 20), ("pool", 16), ("cv", 16)):
            self.slots[q] = [[es.enter_context(nc.semaphore("d_%s%d" % (q, i))), 0] for i in range(n)]
        self.nextslot = {q: 0 for q in self.slots}
        self.seen = {e: {} for e in ("pe", "act", "dve", "pool", "sp")}

    def _semh(self, k):
        if isinstance(k, str):
            return self.sem[k]
        return self.slots[k[0]][k[1]][0]

    def _wait(self, e, deps):
        if e == "cv":
            e = "pool"
        seen = self.seen[e]
        need = {}
        for (k, v) in deps:
            if k == "pe" and e == "pe":
                continue
            if seen.get(k, 0) >= v:
                continue
            if need.get(k, 0) < v:
                need[k] = v
        for k, v in need.items():
            self.eng[e].wait_ge(self._semh(k), v)
            seen[k] = v

    @staticmethod
    def _deps(reads, writes):
        deps = []
        for b in reads:
            if b.w is not None:
                deps.append(b.w)
        for b in writes:
            if b.w is not None:
                deps.append(b.w)
            deps.extend(b.r.items())
        return deps

    def op(self, e, fn, reads=(), writes=()):
        self._wait(e, self._deps(reads, writes))
        ins = fn(self.eng[e])
        self.cnt[e] += 1
        ins.then_inc(self.sem[e], 1)
        c = self.cnt[e]
        for b in reads:
            b.r[e] = c
        for b in writes:
            b.w = (e, c)
            b.r = {}
        return ins

    def dma(self, q, out, in_, reads=(), writes=()):
        i = self.nextslot[q]
        self.nextslot[q] = (i + 1) % len(self.slots[q])
        slot = self.slots[q][i]
        deps = self._deps(reads, writes)
        if slot[1] > 0:
            deps.append(((q, i), 16 * slot[1]))
        self._wait(q, deps)
        ins = self.eng[q].dma_start(out=out, in_=in_)
        slot[1] += 1
        ins.then_inc(slot[0], 16)
        v = 16 * slot[1]
        for b in reads:
            b.r[(q, i)] = v
        for b in writes:
            b.w = ((q, i), v)
            b.r = {}
        return ins

    def barrier(self, include_cv=False):
        evs = [(e, self.cnt[e]) for e in ("pe", "act", "dve", "pool") if self.cnt[e] > 0]
        for q in self.slots:
            if q == "cv" and not include_cv:
                continue
            for i, s in enumerate(self.slots[q]):
                if s[1] > 0:
                    evs.append(((q, i), 16 * s[1]))
        for e in ("pe", "act", "dve", "pool", "sp"):
            self._wait(e, evs)


def lam_init_of(l):
    return 0.8 - 0.6 * math.exp(-0.3 * l)


def build_nc(n_layers=L, dbg=False):
    nc = bass.Bass("TRN2", target_bir_lowering=False)
    es = ExitStack()

    def din(name, shape, dt=F32):
        return nc.dram_tensor(name, list(shape), dt, kind="ExternalInput").ap()

    def dscr(name, shape, dt, out=False):
        if out and dbg:
            return nc.dram_tensor(name, list(shape), dt, kind="ExternalOutput").ap()
        return nc.dram_tensor(name, list(shape), dt).ap()

    xT_in = din("xT", [8, 128, S])
    wih = din("wih", [L, 128, 8, NIN])
    woh = din("woh", [L, 128, 8, D])
    w1h = din("w1h", [L, 8, 128, 8 * 512])
    w2h = din("w2h", [L, 8, 128, 32 * 128])
    g1T = din("g1T", [128, L * 8])
    g2T = din("g2T", [128, L * 8])
    gfT = din("gfT", [128, 8])
    bfT = din("bfT", [8, L])
    lamv = din("lamv", [1, L * 128])
    dng = din("dng", [64, L])
    rbm = din("rbm", [33, 16])
    oha = din("oha", [33, TA_LEN])
    ohc = din("ohc", [33, 3 * TC_LEN])
    outT = nc.dram_tensor("outT", [8, 128, S], F32, kind="ExternalOutput").ap()

    xr = dscr("xr", [8, 128, S], F32, out=True)
    QA = dscr("QA", [256, S], BF16, out=True)
    KA = dscr("KA", [256, S], BF16, out=True)
    QB = dscr("QB", [8, 68, S], BF16, out=True)
    KB = dscr("KB", [8, 68, S], BF16, out=True)
    QC = dscr("QC", [768, S], BF16, out=True)
    KC = dscr("KC", [768, S], BF16, out=True)
    VD = dscr("VD", [24, 128, 32 * 128], BF16, out=True)
    MXD = dscr("MXD", [8, 128, S], BF16, out=True)
    W1b = dscr("W1b", [8, 128, 8 * 512], BF16)
    W2b = dscr("W2b", [8, 128, 32 * 128], BF16)
    TAd = dscr("TAd", [4, TA_LEN], BF16)
    TCd = dscr("TCd", [12, TC_LEN], BF16)

    sc = Sched(nc, es)

    sbn = [0]

    def sb(name, shape, dt):
        sbn[0] += 1
        return es_cur[0].enter_context(nc.sbuf_tensor("%s_%d" % (name, sbn[0]), list(shape), dt))

    es_cur = [es]

    PSall = es.enter_context(nc.psum_tensor("psall", [128, 8 * 512], F32))
    PS = [PSall[:, i * 512:(i + 1) * 512] for i in range(8)]
    PSB = [Buf() for _ in range(8)]

    ident = sb("ident", [128, 128], BF16)
    antid = sb("antid", [128, 128], BF16)
    maskb = sb("maskb", [128, 128], BF16)
    ones_bf = sb("ones_bf", [128, 128], BF16)
    ones_f = sb("ones_f", [128, 64], F32)
    tmpf = sb("tmpf", [128, 128], F32)
    epsc = sb("epsc", [128, 1], F32)
    g1s = sb("g1s", [128, L * 8], F32)
    g2s = sb("g2s", [128, L * 8], F32)
    gfs = sb("gfs", [128, 8], F32)
    nbf = sb("nbf", [8, L], F32)
    lams = sb("lams", [128, 1, L * 128], F32)
    neglam = sb("neglam", [128, L], F32)
    dngs = sb("dngs", [64, L], F32)
    rbs = sb("rbs", [33, 16], F32)
    rb31 = sb("rb31", [128, 1, 4], F32)
    tshA = sb("tshA", [128, 4, 4096], BF16)
    tshC = sb("tshC", [128, 12, 256], BF16)
    B_const = Buf()

    def act_copy(out, in_, scale=1.0):
        return lambda e: e.activation(out=out, in_=in_, func=AF.Copy, scale=scale)

    sc.op("pool", lambda e: e.memset(tmpf[:], 1.0), writes=[B_const])
    sc.op("pool", lambda e: e.affine_select(out=tmpf[:], in_=tmpf[:], pattern=[[-1, 128]],
                                            compare_op=ALU.is_equal, fill=0.0, base=0,
                                            channel_multiplier=1), reads=[B_const], writes=[B_const])
    sc.op("dve", lambda e: e.tensor_copy(out=ident[:], in_=tmpf[:]), reads=[B_const], writes=[B_const])
    sc.op("pool", lambda e: e.memset(tmpf[:], 1.0), writes=[B_const])
    sc.op("pool", lambda e: e.affine_select(out=tmpf[:], in_=tmpf[:], pattern=[[1, 128]],
                                            compare_op=ALU.is_equal, fill=0.0, base=-127,
                                            channel_multiplier=1), reads=[B_const], writes=[B_const])
    sc.op("dve", lambda e: e.tensor_copy(out=antid[:], in_=tmpf[:]), reads=[B_const], writes=[B_const])
    sc.op("pool", lambda e: e.memset(tmpf[:], 0.0), writes=[B_const])
    sc.op("pool", lambda e: e.affine_select(out=tmpf[:], in_=tmpf[:], pattern=[[1, 128]],
                                            compare_op=ALU.is_ge, fill=NEG, base=0,
                                            channel_multiplier=-1), reads=[B_const], writes=[B_const])
    sc.op("dve", lambda e: e.tensor_copy(out=maskb[:], in_=tmpf[:]), reads=[B_const], writes=[B_const])
    sc.op("dve", lambda e: e.memset(ones_bf[:], 1.0), writes=[B_const])
    sc.op("dve", lambda e: e.memset(ones_f[:], 1.0), writes=[B_const])
    sc.op("dve", lambda e: e.memset(epsc[:], EPS), writes=[B_const])

    sc.dma("sp", g1s[:], g1T, writes=[B_const])
    sc.dma("sp", g2s[:], g2T, writes=[B_const])
    sc.dma("sp", gfs[:], gfT, writes=[B_const])
    sc.dma("sp", nbf[:], bfT, writes=[B_const])
    sc.dma("sp", lams[:], lamv.partition_broadcast(128), writes=[B_const])
    sc.dma("sp", dngs[:], dng, writes=[B_const])
    sc.dma("sp", rbs[:], rbm, writes=[B_const])
    sc.dma("sp", rb31[:], rbm[31:32, 0:4].partition_broadcast(128), writes=[B_const])
    sc.op("dve", lambda e: e.tensor_scalar(out=nbf[:], in0=nbf[:], scalar1=-1.0, scalar2=None, op0=ALU.mult),
          reads=[B_const], writes=[B_const])

    with ExitStack() as es1:
        es_cur[0] = es1
        lt = sb("lt", [128, 32], F32)
        lr = sb("lr", [128, 4], F32)
        ohs = sb("ohs", [33, TA_LEN], F32)
        ohcs = sb("ohcs", [33, 3 * TC_LEN], F32)
        tas = sb("tas", [4, TA_LEN], BF16)
        tcs = sb("tcs", [12, TC_LEN], BF16)
        onesrow = sb("onesrow", [8, 2, S], BF16)
        B_t = Buf()
        for l in range(L):
            for j in range(2):
                a0 = l * 128 + j * 64
                sc.op("dve", lambda e, a0=a0: e.tensor_tensor(out=lt[:], in0=lams[:, 0, a0:a0 + 32],
                                                             in1=lams[:, 0, a0 + 32:a0 + 64], op=ALU.mult),
                      reads=[B_const], writes=[B_t])
                sc.op("dve", lambda e, j=j: e.reduce_sum(out=lr[:, j:j + 1], in_=lt[:], axis=mybir.AxisListType.X),
                      reads=[B_t], writes=[B_t])
            sc.op("act", lambda e: e.activation(out=lr[:, 2:4], in_=lr[:, 0:2], func=AF.Exp), reads=[B_t], writes=[B_t])
            sc.op("dve", lambda e, l=l: e.scalar_tensor_tensor(out=neglam[:, l:l + 1], in0=lr[:, 3:4],
                                                              scalar=-lam_init_of(l), in1=lr[:, 2:3],
                                                              op0=ALU.add, op1=ALU.subtract),
                  reads=[B_t], writes=[B_const])
            sc.op("dve", lambda e, l=l: e.tensor_scalar(out=dngs[:, l:l + 1], in0=dngs[:, l:l + 1],
                                                       scalar1=1.0 - lam_init_of(l), scalar2=None, op0=ALU.mult),
                  reads=[B_const], writes=[B_const])
        sc.dma("sp", ohs[:], oha, writes=[B_t])
        sc.dma("sp", ohcs[:], ohc, writes=[B_t])
        for c0 in range(0, TA_LEN, 512):
            n = min(512, TA_LEN - c0)
            sc.op("pe", lambda e, c0=c0, n=n: e.matmul(PS[0][0:4, 0:n], lhsT=rbs[:, 0:4], rhs=ohs[:, c0:c0 + n],
                                                       start=True, stop=True), reads=[B_const, B_t], writes=[PSB[0]])
            sc.op("act", act_copy(tas[:, c0:c0 + n], PS[0][0:4, 0:n], 1.0 / SCALE_A), reads=[PSB[0]], writes=[B_t])
        for g in range(3):
            sc.op("pe", lambda e, g=g: e.matmul(PS[1][0:4, 0:TC_LEN], lhsT=rbs[:, 4 + 4 * g:8 + 4 * g],
                                                rhs=ohcs[:, g * TC_LEN:(g + 1) * TC_LEN], start=True, stop=True),
                  reads=[B_const, B_t], writes=[PSB[1]])
            tg = sb("tg%d" % g, [4, TC_LEN], BF16)
            sc.op("act", act_copy(tg[:], PS[1][0:4, 0:TC_LEN], 1.0 / SCALE_B), reads=[PSB[1]], writes=[B_t])
            sc.dma("pool", TCd[4 * g:4 * g + 4, :], tg[:], reads=[B_t], writes=[B_t])
        sc.dma("pool", TAd, tas[:], reads=[B_t], writes=[B_t])
        for h in range(4):
            sc.dma("sp", tshA[:, h, :], bass.AP(TAd.tensor, h * TA_LEN, [[1, 128], [1, 4096]]),
                   reads=[B_t], writes=[B_const])
        for gh in range(12):
            sc.dma("sp", tshC[:, gh, :], bass.AP(TCd.tensor, gh * TC_LEN, [[1, 128], [1, 256]]),
                   reads=[B_t], writes=[B_const])
        sc.op("dve", lambda e: e.memset(onesrow[:], 1.0), writes=[B_t])
        sc.dma("pool", QB[:, 66:68, :], onesrow[:], reads=[B_t])
        sc.dma("pool", KB[:, 64:66, :], onesrow[:], reads=[B_t])
        sc.barrier()
    es_cur[0] = es

    def norm_a(xc, xcb, sq_t, sq_b):
        sc.op("act", lambda e: e.activation(out=sq_t[:], in_=xc[:], func=AF.Square), reads=[xcb], writes=[sq_b])

    def norm_b(xc, xcb, gcol, rstd_t, rstd_b, sq_t, sq_b, psi, out_fn, out_bufs):
        for c in range(8):
            sc.op("pe", lambda e, c=c: e.matmul(PS[psi][:, :], lhsT=ones_bf[:], rhs=sq_t[:, c, :],
                                                start=(c == 0), stop=(c == 7)),
                  reads=[sq_b, B_const], writes=[PSB[psi]])
        sc.op("act", lambda e: e.activation(out=rstd_t[:], in_=PS[psi][:, :], func=AF.Ln, scale=1.0 / D, bias=epsc[:, 0:1]),
              reads=[PSB[psi], B_const], writes=[rstd_b])
        sc.op("act", lambda e: e.activation(out=rstd_t[:], in_=rstd_t[:], func=AF.Exp, scale=-0.5),
              reads=[rstd_b], writes=[rstd_b])
        for c in range(8):
            sc.op("dve", lambda e, c=c: e.scalar_tensor_tensor(out=out_fn(c), in0=xc[:, c, :], scalar=gcol(c),
                                                             in1=rstd_t[:], op0=ALU.mult, op1=ALU.mult),
                  reads=[xcb, rstd_b, B_const], writes=out_bufs)

    def norm_chunk(xc, xcb, gcol, rstd_t, rstd_b, sq_t, sq_b, psi, out_fn, out_bufs, out_dt_note=None):
        norm_a(xc, xcb, sq_t, sq_b)
        norm_b(xc, xcb, gcol, rstd_t, rstd_b, sq_t, sq_b, psi, out_fn, out_bufs)

    def x_src(l):
        return xT_in if l == 0 else xr

    for l in range(n_layers):
        xsrc = x_src(l)
        B_W1b = [Buf() for _ in range(8)]
        B_W2b = [Buf() for _ in range(8)]

        B_QA = [Buf() for _ in range(2)]
        B_KA = [Buf() for _ in range(2)]
        B_QB = [Buf() for _ in range(8)]
        B_KB = [Buf() for _ in range(8)]
        B_QC = [Buf() for _ in range(6)]
        B_KC = [Buf() for _ in range(6)]
        B_VD = [Buf() for _ in range(24)]
        B_MX = [Buf() for _ in range(8)]
        B_xr = [Buf() for _ in range(8)]

        with ExitStack() as esL:
            es_cur[0] = esL
            hT = sb("hT", [128, 8, S], BF16)
            B_hT = [Buf() for _ in range(8)]
            wbl = [sb("wbl%d" % i, [128, 8, 512], BF16) for i in range(2)]
            wbb = [Buf() for _ in range(2)]
            wf = sb("wf", [128, 8, 8], BF16)
            B_wf = Buf()
            sc.dma("pool", wf[:], wih[l, :, :, F_OFF:F_OFF + 8], writes=[B_wf])
            sc.dma("pool", wbl[0][:, :, 0:512], wih[l, :, :, 0:512], writes=[wbb[0]])
            with ExitStack() as es1:
                es_cur[0] = es1
                xcs = [sb("xc%d" % i, [128, 8, 512], F32) for i in range(3)]
                xcb = [Buf() for _ in range(3)]
                sqs = [sb("sq%d" % i, [128, 8, 512], BF16) for i in range(2)]
                sqb = [Buf() for _ in range(2)]
                rss = [sb("rs%d" % i, [128, 512], F32) for i in range(2)]
                rsb = [Buf() for _ in range(2)]

                def ld(tc):
                    sc.dma("sp", xcs[tc % 3][:], xsrc[:, :, tc * 512:(tc + 1) * 512].rearrange("c p t -> p c t"),
                           writes=[xcb[tc % 3]])
                def na(tc):
                    i = tc % 2
                    norm_a(xcs[tc % 3], xcb[tc % 3], sqs[i], sqb[i])

                def nb(tc):
                    i = tc % 2
                    norm_b(xcs[tc % 3], xcb[tc % 3], lambda c: g1s[:, l * 8 + c:l * 8 + c + 1], rss[i], rsb[i], sqs[i], sqb[i],
                           tc % 2, lambda c, tc=tc: hT[:, c, tc * 512:(tc + 1) * 512], [B_hT[tc]])
                ld(0)
                ld(1)
                ld(2)
                na(0)
                for tc in range(8):
                    if tc + 1 < 8:
                        na(tc + 1)
                    nb(tc)
                    if tc + 3 < 8:
                        ld(tc + 3)
                sc.barrier()
            with ExitStack() as es1:
                es_cur[0] = es1
                eT = sb("eT", [8, S], F32)
                onesf = sb("onesf", [8, S], F32)
                cum = sb("cum", [8, S], F32)
                hi = sb("hi", [8, S], BF16)
                lo = sb("lo", [8, S], BF16)
                nhi = sb("nhi", [8, S], BF16)
                nlo = sb("nlo", [8, S], BF16)
                B_f = Buf()
                sc.op("pool", lambda e: e.memset(onesf[:], 1.0), writes=[B_f])
                for tc in range(8):
                    pi = tc % 2
                    for dc in range(8):
                        sc.op("pe", lambda e, dc=dc, tc=tc, pi=pi: e.matmul(
                            PS[pi][0:8, :], lhsT=wf[:, dc, :], rhs=hT[:, dc, tc * 512:(tc + 1) * 512],
                            start=(dc == 0), stop=(dc == 7)), reads=[B_wf, B_hT[tc]], writes=[PSB[pi]])
                    sc.op("act", lambda e, tc=tc, pi=pi: e.activation(out=eT[:, tc * 512:(tc + 1) * 512], in_=PS[pi][0:8, :],
                                                                     func=AF.Exp, scale=-1.0, bias=nbf[:, l:l + 1]),
                          reads=[PSB[pi], B_const], writes=[B_f])
                sc.op("act", lambda e: e.activation(out=eT[:], in_=eT[:], func=AF.Ln, bias=1.0), reads=[B_f], writes=[B_f])
                sc.op("dve", lambda e: e.tensor_tensor_scan(out=cum[:], data0=onesf[:], data1=eT[:], initial=0.0,
                                                            op0=ALU.mult, op1=ALU.subtract), reads=[B_f], writes=[B_f])
                sc.op("dve", lambda e: e.tensor_scalar(out=cum[:], in0=cum[:], scalar1=8.0, scalar2=None, op0=ALU.mult),
                      reads=[B_f], writes=[B_f])
                sc.op("dve", lambda e: e.tensor_copy(out=hi[:], in_=cum[:]), reads=[B_f], writes=[B_f])
                sc.op("dve", lambda e: e.tensor_tensor(out=eT[:], in0=cum[:], in1=hi[:], op=ALU.subtract), reads=[B_f], writes=[B_f])
                sc.op("dve", lambda e: e.tensor_copy(out=lo[:], in_=eT[:]), reads=[B_f], writes=[B_f])
                sc.op("dve", lambda e: e.tensor_scalar(out=nhi[:], in0=hi[:], scalar1=-1.0, scalar2=None, op0=ALU.mult),
                      reads=[B_f], writes=[B_f])
                sc.op("dve", lambda e: e.tensor_scalar(out=nlo[:], in0=lo[:], scalar1=-1.0, scalar2=None, op0=ALU.mult),
                      reads=[B_f], writes=[B_f])
                sc.dma("pool", QB[:, 64, :], hi[:], reads=[B_f], writes=B_QB)
                sc.dma("pool", QB[:, 65, :], lo[:], reads=[B_f], writes=B_QB)
                sc.dma("pool", KB[:, 66, :], nhi[:], reads=[B_f], writes=B_KB)
                sc.dma("pool", KB[:, 67, :], nlo[:], reads=[B_f], writes=B_KB)
                sc.barrier()
            with ExitStack() as es1:
                es_cur[0] = es1
                stq = [sb("stq%d" % i, [128, S], BF16) for i in range(2)]
                stb = [Buf() for _ in range(2)]
                vst = sb("vst", [128, 4, 32, 128], BF16)
                vsb = Buf()
                sc.op("pool", lambda e: e.memset(vst[:], 1.0), writes=[vsb])

                groups = []
                groups.append(("qk", 0, 512, [("QA", 0, 1), ("QA", 1, 1), ("KA", 0, 1), ("KA", 1, 1)]))
                groups.append(("v", 512, 256, (0, 4, 1)))
                groups.append(("qk", 768, 512, [("QB", b, 1) for b in range(4)]))
                groups.append(("qk", 1280, 512, [("KB", b, 1) for b in range(4)]))
                groups.append(("v", 1792, 256, (4, 4, 1)))
                groups.append(("v", 2048, 256, (8, 4, 1)))
                groups.append(("qk", 2312, 512, [("QC", b, DILS[b // 2]) for b in range(4)]))
                groups.append(("qk", 2824, 512, [("QC", 4, 16), ("QC", 5, 16), ("KC", 0, 1), ("KC", 1, 1)]))
                groups.append(("qk", 3336, 512, [("KC", b, DILS[b // 2]) for b in range(2, 6)]))
                for g in range(3):
                    groups.append(("v", 3848 + 256 * g, 256, (12 + 4 * g, 4, DILS[g])))

                def ldw(gi):
                    kind, c0, n, info = groups[gi]
                    sc.dma("pool", wbl[gi % 2][:, :, 0:n], wih[l, :, :, c0:c0 + n], writes=[wbb[gi % 2]])

                evac_i = [0]
                stq_i = [0]
                psrot = [0]
                for gi, (kind, c0, n, info) in enumerate(groups):
                    if gi + 1 < len(groups):
                        ldw(gi + 1)
                    w = wbl[gi % 2]
                    wb = wbb[gi % 2]
                    if kind == "qk":
                        for lb, (dst, b, dil) in enumerate(info):
                            si = stq_i[0] % 2
                            stq_i[0] += 1
                            st = stq[si]
                            for tc in range(8):
                                pi = psrot[0] % 4
                                psrot[0] += 1
                                for dc in range(8):
                                    sc.op("pe", lambda e, dc=dc, tc=tc, pi=pi, lb=lb: e.matmul(
                                        PS[pi][:, :], lhsT=w[:, dc, lb * 128:(lb + 1) * 128],
                                        rhs=hT[:, dc, tc * 512:(tc + 1) * 512], start=(dc == 0), stop=(dc == 7)),
                                        reads=[wb, B_hT[tc]], writes=[PSB[pi]])
                                if dil == 1:
                                    o_ap = st[:, tc * 512:(tc + 1) * 512]
                                    i_ap = PS[pi][:, :]
                                else:
                                    ni = 512 // dil
                                    o_ap = st[:, :].rearrange("p (r i) -> p r i", r=dil)[:, :, tc * ni:(tc + 1) * ni]
                                    i_ap = PS[pi][:, :].rearrange("p (j r) -> p r j", r=dil)
                                if evac_i[0] % 2 == 0:
                                    sc.op("act", act_copy(o_ap, i_ap), reads=[PSB[pi]], writes=[stb[si]])
                                else:
                                    sc.op("dve", lambda e, o_ap=o_ap, i_ap=i_ap: e.tensor_copy(out=o_ap, in_=i_ap),
                                          reads=[PSB[pi]], writes=[stb[si]])
                                evac_i[0] += 1
                            if dst == "QA":
                                sc.dma("sp", QA[b * 128:(b + 1) * 128, :], st[:], reads=[stb[si]], writes=[B_QA[b]])
                            elif dst == "KA":
                                sc.dma("sp", KA[b * 128:(b + 1) * 128, :], st[:], reads=[stb[si]], writes=[B_KA[b]])
                            elif dst == "QC":
                                sc.dma("sp", QC[b * 128:(b + 1) * 128, :], st[:], reads=[stb[si]], writes=[B_QC[b]])
                            elif dst == "KC":
                                sc.dma("sp", KC[b * 128:(b + 1) * 128, :], st[:], reads=[stb[si]], writes=[B_KC[b]])
                            else:
                                T_ = QB if dst == "QB" else KB
                                BB = B_QB if dst == "QB" else B_KB
                                for hh in range(2):
                                    sc.dma("sp", T_[2 * b + hh, 0:64, :], st[hh * 64:(hh + 1) * 64, :],
                                           reads=[stb[si]], writes=[BB[2 * b + hh]])
                    else:
                        h0, nh, dil = info
                        Lc = S // dil
                        for kb in range(32):
                            pi = psrot[0] % 4
                            psrot[0] += 1
                            r = (kb * 128) // Lc
                            i0 = (kb * 128) % Lc
                            t0 = i0 * dil + r
                            for dc in range(8):
                                sc.op("pe", lambda e, dc=dc, pi=pi, t0=t0, dil=dil, n=n: e.matmul(
                                    PS[pi][:, 0:n], lhsT=hT[:, dc, t0:t0 + 127 * dil + 1:dil], rhs=w[:, dc, 0:n],
                                    start=(dc == 0), stop=(dc == 7)), reads=[wb] + B_hT, writes=[PSB[pi]])
                            o_ap = vst[:, 0:nh, kb, 0:64]
                            i_ap = PS[pi][:, 0:n].rearrange("p (h e) -> p h e", e=64)
                            if evac_i[0] % 2 == 0:
                                sc.op("act", act_copy(o_ap, i_ap), reads=[PSB[pi]], writes=[vsb])
                            else:
                                sc.op("dve", lambda e, o_ap=o_ap, i_ap=i_ap: e.tensor_copy(out=o_ap, in_=i_ap),
                                      reads=[PSB[pi]], writes=[vsb])
                            evac_i[0] += 1
                        for hh in range(nh):
                            sc.dma("sp", VD[h0 + hh], vst[:, hh, :, :].rearrange("p k e -> p (k e)"),
                                   reads=[vsb], writes=[B_VD[h0 + hh]])
                sc.barrier()
        es_cur[0] = es

        with ExitStack() as es1:
            es_cur[0] = es1
            qTs = [sb("qT%d" % i, [128, S], BF16) for i in range(2)]
            kTs = [sb("kT%d" % i, [128, S], BF16) for i in range(2)]
            kT2s = [sb("kTb%d" % i, [128, S], BF16) for i in range(2)]
            vts = [sb("vt%d" % i, [128, 32, 128], BF16) for i in range(2)]
            ldq = [Buf() for _ in range(2)]
            ldk = [Buf() for _ in range(2)]
            ldk2 = [Buf() for _ in range(2)]
            ldv = [Buf() for _ in range(2)]
            for i_ in range(2):
                for t_, b_ in ((qTs[i_], ldq[i_]), (kTs[i_], ldk[i_]), (kT2s[i_], ldk2[i_])):
                    sc.op("dve", lambda e, t_=t_: e.memset(t_[:], 0.0), writes=[b_])
            pts = [sb("pt%d" % i, [128, 1024], BF16) for i in range(4)]
            ptb = [Buf() for _ in range(4)]
            rst = [sb("rst%d" % i, [64, 512], F32) for i in range(2)]
            rstb = [Buf() for _ in range(2)]
            on1 = sb("on1", [64, 512], F32)
            on2 = sb("on2", [64, 512], F32)
            on1b, on2b = Buf(), Buf()
            dds = [sb("dd%d" % i, [64, 512], F32) for i in range(2)]
            sqas = [sb("sqa%d" % i, [64, 512], BF16) for i in range(2)]
            rsa = sb("rsa", [64, 512], F32)
            ddbs = [Buf(), Buf()]
            sqabs = [Buf(), Buf()]
            rsab = Buf()
            mxs = [sb("mxs%d" % i, [64, S], BF16) for i in range(2)]
            mxb = [Buf() for _ in range(2)]
            mxC = sb("mxC", [64, S], BF16)
            mxCb = Buf()
            accC = sb("accC", [128, S], F32)
            accb = Buf()
            SBK = [0, 1, 2, 3]
            OBK = [4, 5, 6, 7]

            heads_ab = [("A", h) for h in range(4)] + [("B", h) for h in range(8)]
            heads_c = [("C", h, g) for h in range(4) for g in range(3)]
            heads = heads_ab + heads_c

            def load_head(hi_, hd):
                i = hi_ % 2
                if hd[0] == "A":
                    h = hd[1]
                    sc.dma("sp", qTs[i][0:64, :], QA[h * 64:(h + 1) * 64, :], reads=[B_QA[h // 2]], writes=[ldq[i]])
                    sc.dma("sp", kTs[i][0:32, :], KA[h * 64:h * 64 + 32, :], reads=[B_KA[h // 2]], writes=[ldk[i]])
                    sc.dma("sp", kT2s[i][32:64, :], KA[h * 64 + 32:h * 64 + 64, :], reads=[B_KA[h // 2]], writes=[ldk2[i]])
                    vi = h
                elif hd[0] == "B":
                    h = hd[1]
                    sc.dma("sp", qTs[i][0:68, :], QB[h], reads=[B_QB[h]], writes=[ldq[i]])
                    sc.dma("sp", kTs[i][0:68, :], KB[h], reads=[B_KB[h]], writes=[ldk[i]])
                    vi = 4 + h
                else:
                    h, g = hd[1], hd[2]
                    r0 = g * 256 + h * 64
                    if hi_ in (12, 13):
                        sc.op("dve", lambda e: e.memset(qTs[i][64:68, :], 0.0), writes=[ldq[i]])
                        sc.op("dve", lambda e: e.memset(kTs[i][64:68, :], 0.0), writes=[ldk[i]])
                    sc.dma("sp", qTs[i][0:64, :], QC[r0:r0 + 64, :], reads=[B_QC[r0 // 128]], writes=[ldq[i]])
                    sc.dma("sp", kTs[i][0:64, :], KC[r0:r0 + 64, :], reads=[B_KC[r0 // 128]], writes=[ldk[i]])
                    vi = 12 + g * 4 + h
                sc.dma("sp", vts[i][:].rearrange("p k e -> p (k e)"), VD[vi], reads=[B_VD[vi]], writes=[ldv[i]])

            cnt_pt = [0]
            cnt_s = [0]
            cnt_fin = [0]
            cnt_mx = [0]

            def normalize(obank, ncols, out_ap, out_buf):
                k = cnt_fin[0] % 2
                cnt_fin[0] += 1
                sc.op("dve", lambda e: e.reciprocal(out=rst[k][0:64, 0:ncols], in_=PS[obank][64:128, 0:ncols]),
                      reads=[PSB[obank]], writes=[rstb[k]])
                sc.op("dve", lambda e: e.tensor_tensor(out=out_ap, in0=PS[obank][0:64, 0:ncols], in1=rst[k][0:64, 0:ncols],
                                                       op=ALU.mult), reads=[PSB[obank], rstb[k]], writes=[out_buf])

            deferred = []

            def tick():
                for d_ in deferred:
                    d_[0] -= 1
                due = [d_ for d_ in deferred if d_[0] <= 0]
                for d_ in due:
                    deferred.remove(d_)
                    d_[1]()

            inflight_tiles = {}

            def busy_banks():
                b_ = set()
                for v_ in inflight_tiles.values():
                    b_ |= v_
                return b_

            POOLS = {"A": [0, 1, 2, 3], "B": [0, 1, 2, 3, 4, 5], "C": [0, 1, 2, 3]}

            def try_single(pool=None):
                pool = pool or POOLS["A"]
                bz = busy_banks()
                for _ in range(len(pool)):
                    b = pool[cnt_s[0] % len(pool)]
                    cnt_s[0] += 1
                    if b not in bz:
                        return b
                return None

            cur_typ = ["A"]

            def take_single():
                b = try_single(POOLS["B"] if cur_typ[0] == "B" else POOLS["A"])
                assert b is not None
                return b

            def try_pair(pool):
                bz = busy_banks()
                npair = len(pool) // 2
                st = (cnt_s[0] % len(pool)) // 2
                for k_ in range(npair):
                    p_ = 2 * ((st + k_) % npair)
                    if not ({p_, p_ + 1} & bz):
                        cnt_s[0] = p_ + 2
                        return p_
                return None

            def head_start_hook(hi_):
                if hi_ + 1 < len(heads):
                    load_head(hi_ + 1, heads[hi_ + 1])
                if 1 <= hi_ <= 8:
                    sc.dma("cv", W1b[hi_ - 1], w1h[l, hi_ - 1], writes=[B_W1b[hi_ - 1]])
                elif 9 <= hi_ <= 16:
                    sc.dma("cv", W2b[hi_ - 9], w2h[l, hi_ - 9], writes=[B_W2b[hi_ - 9]])

            def build_head(hi_, hd):
                i = hi_ % 2
                qT, kT, kT2, vt = qTs[i], kTs[i], kT2s[i], vts[i]
                lqk = [ldq[i], ldk[i], ldk2[i]]
                lv = [ldv[i]]
                typ = hd[0]
                tiles = []

                if typ in ("A", "B"):
                    h = hd[1]
                    ncomp = 2 if typ == "A" else 1
                    scale = SCALE_A if typ == "A" else SCALE_B
                    mi = cnt_mx[0] % 2
                    cnt_mx[0] += 1
                    mx, mxbuf = mxs[mi], mxb[mi]
                    for qc in range(8):
                        if typ == "A":
                            obanks = [OBK[(2 * qc + c) % 4] for c in range(ncomp)]
                        else:
                            obanks = [6 + (qc % 2)]
                        nkb = 4 * qc + 4
                        kb_start = 0
                        if typ == "B":
                            kb_start = 4 * qc
                            for kb2 in range(0, 4 * qc, 2):
                                def qk2(sbk, kb2=kb2, qc=qc):
                                    for u_ in range(2):
                                        sc.op("pe", lambda e: e.matmul(PS[sbk + u_][:, 0:512], lhsT=kT[:, (kb2 + u_) * 128:(kb2 + u_ + 1) * 128],
                                                                       rhs=qT[:, qc * 512:(qc + 1) * 512], start=True, stop=True),
                                              reads=lqk, writes=[PSB[sbk + u_]])

                                def ex2(sbk, pk):
                                    sc.op("act", lambda e: e.activation(out=pts[pk][:, 0:1024], in_=PSall[:, sbk * 512:(sbk + 2) * 512],
                                                                        func=AF.Exp, scale=scale),
                                          reads=[PSB[sbk], PSB[sbk + 1]], writes=[ptb[pk]])

                                def av2(pk, kb2=kb2, nkb=nkb, obanks=obanks):
                                    ob = obanks[0]
                                    for u_ in range(2):
                                        sc.op("pe", lambda e: e.matmul(PS[ob][:, 0:512], lhsT=vt[:, kb2 + u_, :], rhs=pts[pk][:, u_ * 512:(u_ + 1) * 512],
                                                                       start=(kb2 + u_ == 0), stop=False),
                                              reads=[ptb[pk]] + lv, writes=[PSB[ob]])
                                tiles.append((qk2, ex2, av2, None, "pair"))
                        for kb in range(kb_start, nkb):
                            j = kb - 4 * qc
                            col0 = max(0, j) * 128
                            ncol = 512 - col0
                            q0 = qc * 512 + col0
                            for c in range(ncomp):
                                def qk(sbk, kb=kb, col0=col0, ncol=ncol, q0=q0, c=c, j=j):
                                    if typ == "A":
                                        off = q0 - kb * 128
                                        far = (off - 127 >= DTHR)
                                        sc.op("pe", lambda e: e.matmul(PS[sbk][:, col0:512], lhsT=(kT if c == 0 else kT2)[:, kb * 128:(kb + 1) * 128],
                                                                       rhs=qT[:, q0:q0 + ncol], start=True, stop=far),
                                              reads=lqk, writes=[PSB[sbk]])
                                        if not far:
                                            sc.op("pe", lambda e: e.matmul(PS[sbk][:, col0:512], lhsT=antid[:], rhs=tshA[:, h, off:off + ncol],
                                                                           start=False, stop=True), reads=[B_const], writes=[PSB[sbk]])
                                    else:
                                        if j >= 0:
                                            sc.op("pe", lambda e: e.matmul(PS[sbk][:, col0:col0 + 128], lhsT=kT[:, kb * 128:(kb + 1) * 128],
                                                                           rhs=qT[:, q0:q0 + 128], start=True, stop=False),
                                                  reads=lqk, writes=[PSB[sbk]])
                                            sc.op("pe", lambda e: e.matmul(PS[sbk][:, col0:col0 + 128], lhsT=ident[:], rhs=maskb[:],
                                                                           start=False, stop=True), reads=[B_const], writes=[PSB[sbk]])
                                            if ncol > 128:
                                                sc.op("pe", lambda e: e.matmul(PS[sbk][:, col0 + 128:512], lhsT=kT[:, kb * 128:(kb + 1) * 128],
                                                                               rhs=qT[:, q0 + 128:q0 + ncol], start=True, stop=True),
                                                      reads=lqk, writes=[PSB[sbk]])
                                        else:
                                            sc.op("pe", lambda e: e.matmul(PS[sbk][:, 0:512], lhsT=kT[:, kb * 128:(kb + 1) * 128],
                                                                           rhs=qT[:, q0:q0 + 512], start=True, stop=True),
                                                  reads=lqk, writes=[PSB[sbk]])

                                def ex(sbk, pk, col0=col0, far=(typ == "A" and (q0 - kb * 128 - 127 >= DTHR))):
                                    if far:
                                        sc.op("act", lambda e: e.activation(out=pts[pk][:, col0:512], in_=PS[sbk][:, col0:512],
                                                                            func=AF.Exp, scale=scale, bias=rb31[:, 0, h:h + 1]),
                                              reads=[PSB[sbk], B_const], writes=[ptb[pk]])
                                    else:
                                        sc.op("act", lambda e: e.activation(out=pts[pk][:, col0:512], in_=PS[sbk][:, col0:512],
                                                                            func=AF.Exp, scale=scale),
                                              reads=[PSB[sbk]], writes=[ptb[pk]])

                                def av(pk, kb=kb, col0=col0, c=c, nkb=nkb, obanks=obanks):
                                    ob = obanks[c]
                                    sc.op("pe", lambda e: e.matmul(PS[ob][:, col0:512], lhsT=vt[:, kb, :], rhs=pts[pk][:, col0:512],
                                                                   start=(kb == 0), stop=(kb == nkb - 1)),
                                          reads=[ptb[pk]] + lv, writes=[PSB[ob]])

                                post = None
                                if kb == nkb - 1 and c == ncomp - 1:
                                    def post(qc=qc, obanks=obanks):
                                        cs = slice(qc * 512, (qc + 1) * 512)
                                        if typ == "B":
                                            normalize(obanks[0], 512, mx[0:64, cs], mxbuf)
                                        else:
                                            dd, sqa, ddb, sqab = dds[qc % 2], sqas[qc % 2], ddbs[qc % 2], sqabs[qc % 2]
                                            normalize(obanks[0], 512, on1[:], on1b)
                                            normalize(obanks[1], 512, on2[:], on2b)
                                            sc.op("dve", lambda e: e.scalar_tensor_tensor(
                                                out=dd[:], in0=on2[:], scalar=neglam[0:64, l:l + 1], in1=on1[:],
                                                op0=ALU.mult, op1=ALU.add), reads=[on1b, on2b, B_const], writes=[ddb])
                                            sc.op("dve", lambda e: e.tensor_tensor(out=sqa[:], in0=dd[:], in1=dd[:], op=ALU.mult), reads=[ddb], writes=[sqab])

                                            def post2(cs=cs, dd=dd, sqa=sqa, ddb=ddb, sqab=sqab, mx=mx, mxbuf=mxbuf):
                                                MBK = take_single()
                                                sc.op("pe", lambda e: e.matmul(PS[MBK][0:64, :], lhsT=ones_bf[0:64, 0:64], rhs=sqa[:],
                                                                               start=True, stop=True), reads=[sqab, B_const], writes=[PSB[MBK]])
                                                sc.op("act", lambda e: e.activation(out=rsa[:], in_=PS[MBK][0:64, :], func=AF.Ln, scale=1.0 / 64,
                                                                                    bias=epsc[0:64, 0:1]), reads=[PSB[MBK], B_const], writes=[rsab])
                                                sc.op("act", lambda e: e.activation(out=rsa[:], in_=rsa[:], func=AF.Exp, scale=-0.5),
                                                      reads=[rsab], writes=[rsab])
                                                sc.op("dve", lambda e: e.scalar_tensor_tensor(
                                                    out=mx[0:64, cs], in0=dd[:], scalar=dngs[0:64, l:l + 1], in1=rsa[:],
                                                    op0=ALU.mult, op1=ALU.mult), reads=[ddb, rsab, B_const], writes=[mxbuf])
                                            deferred.append([20, post2])
                                tiles.append((qk, ex, av, post, "single"))
                    row0 = (h * 64) if typ == "A" else (256 + h * 64)

                    def head_done(mx=mx, mxbuf=mxbuf, row0=row0):
                        sc.dma("pool", MXD[row0 // 128, row0 % 128:row0 % 128 + 64, :], mx[:], reads=[mxbuf], writes=[B_MX[row0 // 128]])
                else:
                    h, g = hd[1], hd[2]
                    dil = DILS[g]
                    Lc = S // dil
                    gh = g * 4 + h
                    for qc in range(8):
                        ob = OBK[qc % 4]
                        for half in range(2):
                            qb0 = qc * 4 + half * 2
                            hp0 = (qb0 * 128) % Lc != 0
                            cst = 0 if hp0 else 128

                            def qk(sbk, qb0=qb0, hp0=hp0):
                                for u_ in range(2):
                                    qb = qb0 + u_
                                    cb = 256 * u_
                                    if u_ == 1 or hp0:
                                        sc.op("pe", lambda e: e.matmul(PS[sbk][:, cb:cb + 128], lhsT=kT[:, (qb - 1) * 128:qb * 128],
                                                                       rhs=qT[:, qb * 128:(qb + 1) * 128], start=True, stop=False),
                                              reads=lqk, writes=[PSB[sbk]])
                                        sc.op("pe", lambda e: e.matmul(PS[sbk][:, cb:cb + 128], lhsT=antid[:], rhs=tshC[:, gh, 128:256],
                                                                       start=False, stop=True), reads=[B_const], writes=[PSB[sbk]])
                                    sc.op("pe", lambda e: e.matmul(PS[sbk][:, cb + 128:cb + 256], lhsT=kT[:, qb * 128:(qb + 1) * 128],
                                                                   rhs=qT[:, qb * 128:(qb + 1) * 128], start=True, stop=False),
                                          reads=lqk, writes=[PSB[sbk]])
                                    sc.op("pe", lambda e: e.matmul(PS[sbk][:, cb + 128:cb + 256], lhsT=antid[:], rhs=tshC[:, gh, 0:128],
                                                                   start=False, stop=True), reads=[B_const], writes=[PSB[sbk]])

                            def ex(sbk, pk, cst=cst):
                                sc.op("act", lambda e: e.activation(out=pts[pk][:, cst:512], in_=PS[sbk][:, cst:512],
                                                                    func=AF.Exp, scale=SCALE_B),
                                      reads=[PSB[sbk]], writes=[ptb[pk]])

                            def av(pk, qb0=qb0, half=half, hp0=hp0, ob=ob):
                                for u_ in range(2):
                                    qb = qb0 + u_
                                    cb = 256 * u_
                                    oc = slice((half * 2 + u_) * 128, (half * 2 + u_ + 1) * 128)
                                    hp = (u_ == 1 or hp0)
                                    if hp:
                                        sc.op("pe", lambda e: e.matmul(PS[ob][:, oc], lhsT=vt[:, qb - 1, :], rhs=pts[pk][:, cb:cb + 128],
                                                                       start=True, stop=False), reads=[ptb[pk]] + lv, writes=[PSB[ob]])
                                    sc.op("pe", lambda e: e.matmul(PS[ob][:, oc], lhsT=vt[:, qb, :], rhs=pts[pk][:, cb + 128:cb + 256],
                                                                   start=(not hp), stop=True), reads=[ptb[pk]] + lv, writes=[PSB[ob]])

                            post = None
                            if half == 1:
                                def post(qc=qc, ob=ob):
                                    if dil == 1:
                                        sc.op("act", act_copy(accC[:, qc * 512:(qc + 1) * 512], PS[ob][:, :]),
                                              reads=[PSB[ob]], writes=[accb])
                                    else:
                                        nr = max(1, 512 // Lc)
                                        ni = 512 // nr
                                        r0 = (qc * 512) // Lc
                                        i0 = (qc * 512) % Lc
                                        av_ = accC[:, :].rearrange("p (i r) -> p r i", r=dil)[:, r0:r0 + nr, i0:i0 + ni]
                                        pv = PS[ob][:, :].rearrange("p (r i) -> p r i", r=nr)
                                        sc.op("dve", lambda e: e.tensor_tensor(out=av_, in0=av_, in1=pv, op=ALU.add),
                                              reads=[PSB[ob], accb], writes=[accb])
                            tiles.append((qk, ex, av, post, "single"))

                    def head_done(h=h, g=g):
                        if g != 2:
                            return
                        mx, mxbuf = mxC, mxCb
                        for qc in range(8):
                            cs = slice(qc * 512, (qc + 1) * 512)
                            k = cnt_fin[0] % 2
                            cnt_fin[0] += 1
                            sc.op("act", lambda e: e.activation(out=rst[k][0:64, :], in_=accC[64:128, cs], func=AF.Ln),
                                  reads=[accb], writes=[rstb[k]])
                            sc.op("act", lambda e: e.activation(out=rst[k][0:64, :], in_=rst[k][0:64, :], func=AF.Exp, scale=-1.0),
                                  reads=[rstb[k]], writes=[rstb[k]])
                            sc.op("dve", lambda e: e.tensor_tensor(out=mx[0:64, cs], in0=accC[0:64, cs], in1=rst[k][0:64, :],
                                                                   op=ALU.mult), reads=[accb, rstb[k]], writes=[mxbuf])
                        row0 = 768 + h * 64
                        sc.dma("pool", MXD[row0 // 128, row0 % 128:row0 % 128 + 64, :], mx[:], reads=[mxbuf], writes=[B_MX[row0 // 128]])

                return tiles, head_done, typ

            seq = []
            for hi_, hd in enumerate(heads):
                tiles_, hdone_, typ_ = build_head(hi_, hd)
                for ti_, tl_ in enumerate(tiles_):
                    sh_ = (lambda hi_=hi_: head_start_hook(hi_)) if ti_ == 0 else None
                    eh_ = None
                    if ti_ == len(tiles_) - 1:
                        if typ_ == "A":
                            eh_ = (lambda hdone_=hdone_: deferred.append([22, hdone_]))
                        else:
                            eh_ = hdone_
                    seq.append(tl_[0:4] + (sh_, eh_, tl_[4], typ_))
            load_head(0, heads[0])
            LA = 2
            n = len(seq)
            assign = {}
            def consume(u):
                pk = cnt_pt[0] % 4
                cnt_pt[0] += 1
                seq[u][1](assign[u], pk)
                seq[u][2](pk)
                inflight_tiles.pop(u, None)
                if seq[u][3] is not None:
                    seq[u][3]()
                if seq[u][4] is not None:
                    seq[u][4]()
                if seq[u][5] is not None:
                    seq[u][5]()
                tick()

            def try_issue(t):
                pool = POOLS[seq[t][7]]
                if seq[t][6] == "pair":
                    sbk = try_pair(pool)
                    banks = None if sbk is None else {sbk, sbk + 1}
                else:
                    sbk = try_single(pool)
                    banks = None if sbk is None else {sbk}
                if sbk is None:
                    return False
                inflight_tiles[t] = banks
                cur_typ[0] = seq[t][7]
                assign[t] = sbk
                seq[t][0](sbk)
                return True

            for t in range(n + LA):
                u = t - LA
                issued = (t < n) and try_issue(t)
                if u >= 0:
                    consume(u)
                if t < n and not issued:
                    ok_ = try_issue(t)
                    assert ok_
            while deferred:
                deferred.pop(0)[1]()
            sc.barrier()
        es_cur[0] = es

        with ExitStack() as es1:
            es_cur[0] = es1
            wo = sb("wo", [128, 8, D], BF16)
            wob = Buf()
            sc.dma("pool", wo[:], woh[l], writes=[wob])
            mxc = [sb("mxc%d" % i, [128, 8, 512], BF16) for i in range(2)]
            mxcb = [Buf() for _ in range(2)]
            xcs = [sb("xf%d" % i, [128, 8, 512], F32) for i in range(2)]
            xcb = [Buf() for _ in range(2)]
            sq = sb("sqf", [128, 8, 512], BF16)
            sqb_ = Buf()
            rs2 = sb("rs2", [128, 512], F32)
            rs2b = Buf()
            h2s = [sb("h2_%d" % i, [128, 8, 512], BF16) for i in range(2)]
            h2bs = [Buf() for _ in range(2)]
            uT = sb("uT", [128, 32, 512], BF16)
            uTb = [Buf() for _ in range(32)]
            rl = [sb("rl%d" % i, [128, 512], F32) for i in range(2)]
            rlb = [Buf() for _ in range(2)]
            w1s = [sb("w1s%d" % i, [128, 8, 512], BF16) for i in range(2)]
            w1sb = [Buf() for _ in range(2)]
            w2s = [sb("w2s%d" % i, [128, 32, 128], BF16) for i in range(2)]
            w2sb = [Buf() for _ in range(2)]
            last = (l == n_layers - 1)

            def ldx(tc):
                i = tc % 2
                sc.dma("sp", xcs[i][:], xsrc[:, :, tc * 512:(tc + 1) * 512].rearrange("c p t -> p c t"),
                       reads=[B_xr[tc]], writes=[xcb[i]])
                sc.dma("sp", mxc[i][:], MXD[:, :, tc * 512:(tc + 1) * 512].rearrange("c p t -> p c t"),
                       reads=B_MX, writes=[mxcb[i]])

            wiss = [0, 0]

            def ensure_w1(n):
                while wiss[0] <= min(n, 63):
                    m = wiss[0]
                    sc.dma("sp", w1s[m % 2][:].rearrange("p a b -> p (a b)"), W1b[m % 8], reads=[B_W1b[m % 8]], writes=[w1sb[m % 2]])
                    wiss[0] += 1

            def ensure_w2(n):
                while wiss[1] <= min(n, 63):
                    m = wiss[1]
                    sc.dma("sp", w2s[m % 2][:].rearrange("p a b -> p (a b)"), W2b[m % 8], reads=[B_W2b[m % 8]], writes=[w2sb[m % 2]])
                    wiss[1] += 1

            prot = [0]

            def pre(tc):
                i = tc % 2
                xc, xb = xcs[i], xcb[i]
                for dc in range(8):
                    pi = prot[0] % 4
                    prot[0] += 1
                    for fc in range(8):
                        sc.op("pe", lambda e, dc=dc, fc=fc, pi=pi: e.matmul(
                            PS[pi][:, :], lhsT=wo[:, fc, dc * 128:(dc + 1) * 128], rhs=mxc[i][:, fc, :],
                            start=(fc == 0), stop=(fc == 7)), reads=[wob, mxcb[i]], writes=[PSB[pi]])
                    sc.op("dve", lambda e, dc=dc, pi=pi: e.tensor_tensor(out=xc[:, dc, :], in0=xc[:, dc, :], in1=PS[pi][:, :], op=ALU.add),
                          reads=[PSB[pi], xb], writes=[xb])
                norm_a(xc, xb, sq, sqb_)

            def pre_b(tc):
                i = tc % 2
                xc, xb = xcs[i], xcb[i]
                norm_b(xc, xb, lambda c: g2s[:, l * 8 + c:l * 8 + c + 1], rs2, rs2b, sq, sqb_, 4,
                       lambda c: h2s[i][:, c, :], [h2bs[i]])

            def up(tc):
                h2, h2b = h2s[tc % 2], h2bs[tc % 2]
                for fb in range(8):
                    n = tc * 8 + fb
                    ensure_w1(n + 1)
                    if fb == 6:
                        ensure_w2(tc * 8)
                    if fb == 2 and tc >= 1 and tc + 1 < 8:
                        ldx(tc + 1)
                    cur = n % 2
                    for fl in range(4):
                        fc = fb * 4 + fl
                        pi = prot[0] % 4
                        prot[0] += 1
                        for dc in range(8):
                            sc.op("pe", lambda e, dc=dc, fl=fl, pi=pi, cur=cur: e.matmul(
                                PS[pi][:, :], lhsT=w1s[cur][:, dc, fl * 128:(fl + 1) * 128], rhs=h2[:, dc, :],
                                start=(dc == 0), stop=(dc == 7)), reads=[w1sb[cur], h2b], writes=[PSB[pi]])
                        k = fc % 2
                        sc.op("act", lambda e, pi=pi, k=k: e.activation(out=rl[k][:], in_=PS[pi][:, :], func=AF.Relu),
                              reads=[PSB[pi]], writes=[rlb[k]])
                        eng = "pool" if fc % 2 == 0 else "dve"
                        sc.op(eng, lambda e, fc=fc, k=k: e.tensor_tensor(out=uT[:, fc, :], in0=rl[k][:], in1=rl[k][:], op=ALU.mult),
                              reads=[rlb[k]], writes=[uTb[fc]])

            def down(tc, hook=None):
                i = tc % 2
                xc, xb = xcs[i], xcb[i]
                for dc in range(8):
                    if dc == 2 and hook is not None:
                        hook()
                    n = tc * 8 + dc
                    ensure_w2(n + 1)
                    cur = n % 2
                    pi = prot[0] % 4
                    prot[0] += 1
                    for fc in range(32):
                        sc.op("pe", lambda e, fc=fc, pi=pi, cur=cur: e.matmul(
                            PS[pi][:, :], lhsT=w2s[cur][:, fc, :], rhs=uT[:, fc, :],
                            start=(fc == 0), stop=(fc == 31)), reads=[w2sb[cur], uTb[fc]], writes=[PSB[pi]])
                    sc.op("dve", lambda e, dc=dc, pi=pi: e.tensor_tensor(out=xc[:, dc, :], in0=xc[:, dc, :], in1=PS[pi][:, :], op=ALU.add),
                          reads=[PSB[pi], xb], writes=[xb])
                if not last:
                    sc.dma("pool", xr[:, :, tc * 512:(tc + 1) * 512].rearrange("c p t -> p c t"), xc[:], reads=[xb], writes=[B_xr[tc]])
                else:
                    norm_chunk(xc, xb, lambda c: gfs[:, c:c + 1], rs2, rs2b, sq, sqb_, 4,
                               lambda c: xc[:, c, :], [xb])
                    sc.dma("pool", outT[:, :, tc * 512:(tc + 1) * 512].rearrange("c p t -> p c t"), xc[:], reads=[xb], writes=[B_xr[tc]])

            ldx(0)
            ldx(1)
            ensure_w1(0)
            pre(0)
            pre_b(0)
            for tc in range(8):
                up(tc)
                if tc + 1 < 8:
                    pre(tc + 1)
                    down(tc, hook=lambda tc=tc: pre_b(tc + 1))
                else:
                    down(tc)
            sc.barrier(include_cv=True)
        es_cur[0] = es

    sc.barrier(include_cv=True)
    es.close()
    return nc


def _t5_bucket(d):
    d = np.maximum(d, 0)
    df = np.maximum(d, 1).astype(np.float32)
    large = 16 + (np.log(df / np.float32(16)) / np.float32(math.log(2048 / 16)) * np.float32(16)).astype(np.int32)
    large = np.minimum(large, 31)
    return np.where(d < 16, d, large)


def _consts():
    oha = np.zeros((33, TA_LEN), np.float32)
    i = np.arange(TA_LEN)
    dist = i - 127
    bk = _t5_bucket(dist)
    ok = dist >= 0
    oha[bk[ok], i[ok]] = 1.0
    oha[32, i[~ok]] = 1.0
    ohc = np.zeros((33, 3 * TC_LEN), np.float32)
    for g, dil in enumerate(DILS):
        i = np.arange(TC_LEN)
        jj = i - 127
        ok = (jj >= 0) & (jj <= 128)
        bk = _t5_bucket(jj * dil)
        ohc[bk[ok], g * TC_LEN + i[ok]] = 1.0
        ohc[32, g * TC_LEN + i[~ok]] = 1.0
    return oha, ohc


DTHR = int(np.nonzero(_t5_bucket(np.arange(4096)) == 31)[0][0])
_NC_CACHE = {}


def _prep_shared(inp):
    f = np.float32
    w_in = np.asarray(inp["w_in"], f)
    w_o = np.asarray(inp["w_o"], f)
    w_1 = np.asarray(inp["w_1"], f)
    w_2 = np.asarray(inp["w_2"], f)
    sh = {}
    sh["wih"] = np.ascontiguousarray(w_in.reshape(L, 8, 128, NIN).transpose(0, 2, 1, 3))
    sh["woh"] = np.ascontiguousarray(w_o.reshape(L, 8, 128, D).transpose(0, 2, 1, 3))
    sh["w1h"] = np.ascontiguousarray(w_1.reshape(L, 8, 128, 8, 512).transpose(0, 3, 2, 1, 4)).reshape(L, 8, 128, 8 * 512)
    sh["w2h"] = np.ascontiguousarray(w_2.reshape(L, 32, 128, 8, 128).transpose(0, 3, 2, 1, 4)).reshape(L, 8, 128, 32 * 128)
    sh["g1T"] = np.ascontiguousarray(np.asarray(inp["norm1_g"], f).reshape(L, 8, 128).transpose(2, 0, 1)).reshape(128, L * 8)
    sh["g2T"] = np.ascontiguousarray(np.asarray(inp["norm2_g"], f).reshape(L, 8, 128).transpose(2, 0, 1)).reshape(128, L * 8)
    sh["gfT"] = np.ascontiguousarray(np.asarray(inp["final_g"], f).reshape(8, 128).T)
    sh["bfT"] = np.ascontiguousarray(np.asarray(inp["b_f"], f).T)
    lam = np.stack([np.asarray(inp[k], f) for k in ("lam_q1", "lam_k1", "lam_q2", "lam_k2")], axis=1)
    sh["lamv"] = np.ascontiguousarray(lam.reshape(1, L * 128))
    sh["dng"] = np.ascontiguousarray(np.asarray(inp["diff_norm_g"], f).T)
    rbm = np.full((33, 16), NEG, f)
    rbm[:32] = np.asarray(inp["rel_bias"], f)
    sh["rbm"] = rbm
    oha, ohc = _consts()
    sh["oha"] = oha
    sh["ohc"] = ohc
    return sh


def kernel(**inputs):
    x = np.asarray(inputs["x"], np.float32)
    nb = x.shape[0]
    sh = _prep_shared(inputs)
    if "nc" not in _NC_CACHE:
        _NC_CACHE["nc"] = build_nc()
    nc = _NC_CACHE["nc"]
    in_maps = []
    for b in range(nb):
        m = dict(sh)
        m["xT"] = np.ascontiguousarray(x[b].T).reshape(8, 128, S)
        in_maps.append(m)
    res = run_bass_kernel_spmd(nc, in_maps, core_ids=list(range(nb)))
    out = np.empty((nb, S, D), np.float32)
    for b in range(nb):
        out[b] = res.results[b]["outT"].reshape(D, S).T
    return out
```

```python
import math
from contextlib import ExitStack
import numpy as np
import concourse.bass as bass
import concourse.mybir as mybir
from concourse.bass_utils import run_bass_kernel_spmd

F32 = mybir.dt.float32
BF16 = mybir.dt.bfloat16
AF = mybir.ActivationFunctionType
ALU = mybir.AluOpType

S = 4096
D = 1024
L = 4
NIN = 4616
DFF = 4096
EPS = 1e-6
A_OFF, B_OFF, F_OFF, C_OFF = 0, 768, 2304, 2312
DILS = (1, 4, 16)
SCALE_A = 32 ** -0.5
SCALE_B = 0.125
NEG = -30000.0
TA_LEN = 4352
DTHR = 1600
TC_LEN = 512


class Buf:
    __slots__ = ("w", "r")

    def __init__(self):
        self.w = None
        self.r = {}


class Sched:
    def __init__(self, nc, es):
        self.nc = nc
        self.eng = {"pe": nc.tensor, "act": nc.scalar, "dve": nc.vector,
                    "pool": nc.gpsimd, "sp": nc.sync, "cv": nc.gpsimd, "aq": nc.scalar}
        self.sem = {}
        self.cnt = {}
        for e in ("pe", "act", "dve", "pool"):
            self.sem[e] = es.enter_context(nc.semaphore("s_" + e))
            self.cnt[e] = 0
        self.slots = {}
        for q, n in (("sp", 20), ("pool", 16), ("cv", 16), ("aq", 6)):
            self.slots[q] = [[es.enter_context(nc.semaphore("d_%s%d" % (q, i))), 0] for i in range(n)]
        self.nextslot = {q: 0 for q in self.slots}
        self.seen = {e: {} for e in ("pe", "act", "dve", "pool", "sp")}

    def _semh(self, k):
        if isinstance(k, str):
            return self.sem[k]
        return self.slots[k[0]][k[1]][0]

    def _wait(self, e, deps):
        if e == "cv":
            e = "pool"
        if e == "aq":
            e = "act"
        seen = self.seen[e]
        need = {}
        for (k, v) in deps:
            if k == "pe" and e == "pe":
                continue
            if seen.get(k, 0) >= v:
                continue
            if need.get(k, 0) < v:
                need[k] = v
        for k, v in need.items():
            self.eng[e].wait_ge(self._semh(k), v)
            seen[k] = v

    @staticmethod
    def _deps(reads, writes):
        deps = []
        for b in reads:
            if b.w is not None:
                deps.append(b.w)
        for b in writes:
            if b.w is not None:
                deps.append(b.w)
            deps.extend(b.r.items())
        return deps

    def op(self, e, fn, reads=(), writes=()):
        self._wait(e, self._deps(reads, writes))
        ins = fn(self.eng[e])
        self.cnt[e] += 1
        ins.then_inc(self.sem[e], 1)
        c = self.cnt[e]
        for b in reads:
            b.r[e] = c
        for b in writes:
            b.w = (e, c)
            b.r = {}
        return ins

    def dma(self, q, out, in_, reads=(), writes=()):
        i = self.nextslot[q]
        self.nextslot[q] = (i + 1) % len(self.slots[q])
        slot = self.slots[q][i]
        deps = self._deps(reads, writes)
        if slot[1] > 0:
            deps.append(((q, i), 16 * slot[1]))
        self._wait(q, deps)
        ins = self.eng[q].dma_start(out=out, in_=in_)
        slot[1] += 1
        ins.then_inc(slot[0], 16)
        v = 16 * slot[1]
        for b in reads:
            b.r[(q, i)] = v
        for b in writes:
            b.w = ((q, i), v)
            b.r = {}
        return ins

    def barrier(self, include_cv=False):
        evs = [(e, self.cnt[e]) for e in ("pe", "act", "dve", "pool") if self.cnt[e] > 0]
        for q in self.slots:
            if q == "cv" and not include_cv:
                continue
            for i, s in enumerate(self.slots[q]):
                if s[1] > 0:
                    evs.append(((q, i), 16 * s[1]))
        for e in ("pe", "act", "dve", "pool", "sp"):
            self._wait(e, evs)


def lam_init_of(l):
    return 0.8 - 0.6 * math.exp(-0.3 * l)


def build_nc(n_layers=L, dbg=False):
    nc = bass.Bass("TRN2", target_bir_lowering=False)
    es = ExitStack()

    def din(name, shape, dt=F32):
        return nc.dram_tensor(name, list(shape), dt, kind="ExternalInput").ap()

    def dscr(name, shape, dt, out=False):
        if out and dbg:
            return nc.dram_tensor(name, list(shape), dt, kind="ExternalOutput").ap()
        return nc.dram_tensor(name, list(shape), dt).ap()

    xT_in = din("xT", [8, 128, S])
    wih = din("wih", [L, 128, 8, NIN])
    woh = din("woh", [L, 128, 8, D])
    w1h = din("w1h", [L, 8, 128, 8 * 512])
    w2h = din("w2h", [L, 8, 128, 32 * 128])
    g1T = din("g1T", [128, L * 8])
    g2T = din("g2T", [128, L * 8])
    gfT = din("gfT", [128, 8])
    bfT = din("bfT", [8, L])
    lamv = din("lamv", [1, L * 128])
    dng = din("dng", [64, L])
    rbm = din("rbm", [33, 16])
    oha = din("oha", [33, TA_LEN])
    ohc = din("ohc", [33, 3 * TC_LEN])
    outT = nc.dram_tensor("outT", [8, 128, S], F32, kind="ExternalOutput").ap()

    xr = dscr("xr", [8, 128, S], F32, out=True)
    QA = dscr("QA", [256, S], BF16, out=True)
    KA = dscr("KA", [256, S], BF16, out=True)
    QB = dscr("QB", [8, 68, S], BF16, out=True)
    KB = dscr("KB", [8, 68, S], BF16, out=True)
    QC = dscr("QC", [768, S], BF16, out=True)
    KC = dscr("KC", [768, S], BF16, out=True)
    VD = dscr("VD", [24, 128, 32 * 128], BF16, out=True)
    MXD = dscr("MXD", [8, 128, S], BF16, out=True)
    W1b = dscr("W1b", [8, 128, 8 * 512], BF16)
    W2b = dscr("W2b", [8, 128, 32 * 128], BF16)
    TAd = dscr("TAd", [4, TA_LEN], BF16)
    TCd = dscr("TCd", [12, TC_LEN], BF16)

    sc = Sched(nc, es)

    sbn = [0]

    def sb(name, shape, dt):
        sbn[0] += 1
        return es_cur[0].enter_context(nc.sbuf_tensor("%s_%d" % (name, sbn[0]), list(shape), dt))

    es_cur = [es]

    PSall = es.enter_context(nc.psum_tensor("psall", [128, 8 * 512], F32))
    PS = [PSall[:, i * 512:(i + 1) * 512] for i in range(8)]
    PSB = [Buf() for _ in range(8)]

    ident = sb("ident", [128, 128], BF16)
    antid = sb("antid", [128, 128], BF16)
    maskb = sb("maskb", [128, 128], BF16)
    ones_bf = sb("ones_bf", [128, 128], BF16)
    ones_f = sb("ones_f", [128, 64], F32)
    tmpf = sb("tmpf", [128, 128], F32)
    epsc = sb("epsc", [128, 1], F32)
    g1s = sb("g1s", [128, L * 8], F32)
    g2s = sb("g2s", [128, L * 8], F32)
    gfs = sb("gfs", [128, 8], F32)
    nbf = sb("nbf", [8, L], F32)
    lams = sb("lams", [128, 1, L * 128], F32)
    neglam = sb("neglam", [128, L], F32)
    dngs = sb("dngs", [64, L], F32)
    rbs = sb("rbs", [33, 16], F32)
    rb31 = sb("rb31", [128, 1, 4], F32)
    tshA = sb("tshA", [128, 4, 4096], BF16)
    tshC = sb("tshC", [128, 12, 256], BF16)
    B_const = Buf()

    def act_copy(out, in_, scale=1.0):
        return lambda e: e.activation(out=out, in_=in_, func=AF.Copy, scale=scale)

    sc.op("pool", lambda e: e.memset(tmpf[:], 1.0), writes=[B_const])
    sc.op("pool", lambda e: e.affine_select(out=tmpf[:], in_=tmpf[:], pattern=[[-1, 128]],
                                            compare_op=ALU.is_equal, fill=0.0, base=0,
                                            channel_multiplier=1), reads=[B_const], writes=[B_const])
    sc.op("dve", lambda e: e.tensor_copy(out=ident[:], in_=tmpf[:]), reads=[B_const], writes=[B_const])
    sc.op("pool", lambda e: e.memset(tmpf[:], 1.0), writes=[B_const])
    sc.op("pool", lambda e: e.affine_select(out=tmpf[:], in_=tmpf[:], pattern=[[1, 128]],
                                            compare_op=ALU.is_equal, fill=0.0, base=-127,
                                            channel_multiplier=1), reads=[B_const], writes=[B_const])
    sc.op("dve", lambda e: e.tensor_copy(out=antid[:], in_=tmpf[:]), reads=[B_const], writes=[B_const])
    sc.op("pool", lambda e: e.memset(tmpf[:], 0.0), writes=[B_const])
    sc.op("pool", lambda e: e.affine_select(out=tmpf[:], in_=tmpf[:], pattern=[[1, 128]],
                                            compare_op=ALU.is_ge, fill=NEG, base=0,
                                            channel_multiplier=-1), reads=[B_const], writes=[B_const])
    sc.op("dve", lambda e: e.tensor_copy(out=maskb[:], in_=tmpf[:]), reads=[B_const], writes=[B_const])
    sc.op("dve", lambda e: e.memset(ones_bf[:], 1.0), writes=[B_const])
    sc.op("dve", lambda e: e.memset(ones_f[:], 1.0), writes=[B_const])
    sc.op("dve", lambda e: e.memset(epsc[:], EPS), writes=[B_const])

    sc.dma("sp", g1s[:], g1T, writes=[B_const])
    sc.dma("sp", g2s[:], g2T, writes=[B_const])
    sc.dma("sp", gfs[:], gfT, writes=[B_const])
    sc.dma("sp", nbf[:], bfT, writes=[B_const])
    sc.dma("sp", lams[:], lamv.partition_broadcast(128), writes=[B_const])
    sc.dma("sp", dngs[:], dng, writes=[B_const])
    sc.dma("sp", rbs[:], rbm, writes=[B_const])
    sc.dma("sp", rb31[:], rbm[31:32, 0:4].partition_broadcast(128), writes=[B_const])
    sc.op("dve", lambda e: e.tensor_scalar(out=nbf[:], in0=nbf[:], scalar1=-1.0, scalar2=None, op0=ALU.mult),
          reads=[B_const], writes=[B_const])

    with ExitStack() as es1:
        es_cur[0] = es1
        lt = sb("lt", [128, 32], F32)
        lr = sb("lr", [128, 4], F32)
        ohs = sb("ohs", [33, TA_LEN], F32)
        ohcs = sb("ohcs", [33, 3 * TC_LEN], F32)
        tas = sb("tas", [4, TA_LEN], BF16)
        tcs = sb("tcs", [12, TC_LEN], BF16)
        onesrow = sb("onesrow", [8, 2, S], BF16)
        B_t = Buf()
        for l in range(L):
            for j in range(2):
                a0 = l * 128 + j * 64
                sc.op("dve", lambda e, a0=a0: e.tensor_tensor(out=lt[:], in0=lams[:, 0, a0:a0 + 32],
                                                             in1=lams[:, 0, a0 + 32:a0 + 64], op=ALU.mult),
                      reads=[B_const], writes=[B_t])
                sc.op("dve", lambda e, j=j: e.reduce_sum(out=lr[:, j:j + 1], in_=lt[:], axis=mybir.AxisListType.X),
                      reads=[B_t], writes=[B_t])
            sc.op("act", lambda e: e.activation(out=lr[:, 2:4], in_=lr[:, 0:2], func=AF.Exp), reads=[B_t], writes=[B_t])
            sc.op("dve", lambda e, l=l: e.scalar_tensor_tensor(out=neglam[:, l:l + 1], in0=lr[:, 3:4],
                                                              scalar=-lam_init_of(l), in1=lr[:, 2:3],
                                                              op0=ALU.add, op1=ALU.subtract),
                  reads=[B_t], writes=[B_const])
            sc.op("dve", lambda e, l=l: e.tensor_scalar(out=dngs[:, l:l + 1], in0=dngs[:, l:l + 1],
                                                       scalar1=1.0 - lam_init_of(l), scalar2=None, op0=ALU.mult),
                  reads=[B_const], writes=[B_const])
        sc.dma("sp", ohs[:], oha, writes=[B_t])
        sc.dma("sp", ohcs[:], ohc, writes=[B_t])
        for c0 in range(0, TA_LEN, 512):
            n = min(512, TA_LEN - c0)
            sc.op("pe", lambda e, c0=c0, n=n: e.matmul(PS[0][0:4, 0:n], lhsT=rbs[:, 0:4], rhs=ohs[:, c0:c0 + n],
                                                       start=True, stop=True), reads=[B_const, B_t], writes=[PSB[0]])
            sc.op("act", act_copy(tas[:, c0:c0 + n], PS[0][0:4, 0:n], 1.0 / SCALE_A), reads=[PSB[0]], writes=[B_t])
        for g in range(3):
            sc.op("pe", lambda e, g=g: e.matmul(PS[1][0:4, 0:TC_LEN], lhsT=rbs[:, 4 + 4 * g:8 + 4 * g],
                                                rhs=ohcs[:, g * TC_LEN:(g + 1) * TC_LEN], start=True, stop=True),
                  reads=[B_const, B_t], writes=[PSB[1]])
            tg = sb("tg%d" % g, [4, TC_LEN], BF16)
            sc.op("act", act_copy(tg[:], PS[1][0:4, 0:TC_LEN], 1.0 / SCALE_B), reads=[PSB[1]], writes=[B_t])
            sc.dma("pool", TCd[4 * g:4 * g + 4, :], tg[:], reads=[B_t], writes=[B_t])
        sc.dma("pool", TAd, tas[:], reads=[B_t], writes=[B_t])
        for h in range(4):
            sc.dma("sp", tshA[:, h, :], bass.AP(TAd.tensor, h * TA_LEN, [[1, 128], [1, 4096]]),
                   reads=[B_t], writes=[B_const])
        for gh in range(12):
            sc.dma("sp", tshC[:, gh, :], bass.AP(TCd.tensor, gh * TC_LEN, [[1, 128], [1, 256]]),
                   reads=[B_t], writes=[B_const])
        sc.op("dve", lambda e: e.memset(onesrow[:], 1.0), writes=[B_t])
        sc.dma("pool", QB[:, 66:68, :], onesrow[:], reads=[B_t])
        sc.dma("pool", KB[:, 64:66, :], onesrow[:], reads=[B_t])
        sc.barrier()
    es_cur[0] = es

    def norm_a(xc, xcb, sq_t, sq_b):
        sc.op("act", lambda e: e.activation(out=sq_t[:], in_=xc[:], func=AF.Square), reads=[xcb], writes=[sq_b])

    def norm_b(xc, xcb, gcol, rstd_t, rstd_b, sq_t, sq_b, psi, out_fn, out_bufs):
        for c in range(8):
            sc.op("pe", lambda e, c=c: e.matmul(PS[psi][:, :], lhsT=ones_bf[:], rhs=sq_t[:, c, :],
                                                start=(c == 0), stop=(c == 7)),
                  reads=[sq_b, B_const], writes=[PSB[psi]])
        sc.op("act", lambda e: e.activation(out=rstd_t[:], in_=PS[psi][:, :], func=AF.Ln, scale=1.0 / D, bias=epsc[:, 0:1]),
              reads=[PSB[psi], B_const], writes=[rstd_b])
        sc.op("act", lambda e: e.activation(out=rstd_t[:], in_=rstd_t[:], func=AF.Exp, scale=-0.5),
              reads=[rstd_b], writes=[rstd_b])
        for c in range(8):
            sc.op("dve", lambda e, c=c: e.scalar_tensor_tensor(out=out_fn(c), in0=xc[:, c, :], scalar=gcol(c),
                                                             in1=rstd_t[:], op0=ALU.mult, op1=ALU.mult),
                  reads=[xcb, rstd_b, B_const], writes=out_bufs)

    def norm_chunk(xc, xcb, gcol, rstd_t, rstd_b, sq_t, sq_b, psi, out_fn, out_bufs, out_dt_note=None):
        norm_a(xc, xcb, sq_t, sq_b)
        norm_b(xc, xcb, gcol, rstd_t, rstd_b, sq_t, sq_b, psi, out_fn, out_bufs)

    def x_src(l):
        return xT_in if l == 0 else xr

    for l in range(n_layers):
        xsrc = x_src(l)
        B_W1b = [Buf() for _ in range(8)]
        B_W2b = [Buf() for _ in range(8)]

        B_QA = [Buf() for _ in range(2)]
        B_KA = [Buf() for _ in range(2)]
        B_QB = [Buf() for _ in range(8)]
        B_KB = [Buf() for _ in range(8)]
        B_QC = [Buf() for _ in range(6)]
        B_KC = [Buf() for _ in range(6)]
        B_VD = [Buf() for _ in range(24)]
        B_MX = [Buf() for _ in range(8)]
        B_xr = [Buf() for _ in range(8)]

        with ExitStack() as esL:
            es_cur[0] = esL
            hT = sb("hT", [128, 8, S], BF16)
            B_hT = [Buf() for _ in range(8)]
            wbl = [sb("wbl%d" % i, [128, 8, 512], BF16) for i in range(2)]
            wbb = [Buf() for _ in range(2)]
            wf = sb("wf", [128, 8, 8], BF16)
            B_wf = Buf()
            sc.dma("pool", wf[:], wih[l, :, :, F_OFF:F_OFF + 8], writes=[B_wf])
            sc.dma("pool", wbl[0][:, :, 0:512], wih[l, :, :, 0:512], writes=[wbb[0]])
            with ExitStack() as es1:
                es_cur[0] = es1
                xcs = [sb("xc%d" % i, [128, 8, 512], F32) for i in range(3)]
                xcb = [Buf() for _ in range(3)]
                sqs = [sb("sq%d" % i, [128, 8, 512], BF16) for i in range(2)]
                sqb = [Buf() for _ in range(2)]
                rss = [sb("rs%d" % i, [128, 512], F32) for i in range(2)]
                rsb = [Buf() for _ in range(2)]

                def ld(tc):
                    sc.dma("sp", xcs[tc % 3][:], xsrc[:, :, tc * 512:(tc + 1) * 512].rearrange("c p t -> p c t"),
                           writes=[xcb[tc % 3]])
                def na(tc):
                    i = tc % 2
                    norm_a(xcs[tc % 3], xcb[tc % 3], sqs[i], sqb[i])

                def nb(tc):
                    i = tc % 2
                    norm_b(xcs[tc % 3], xcb[tc % 3], lambda c: g1s[:, l * 8 + c:l * 8 + c + 1], rss[i], rsb[i], sqs[i], sqb[i],
                           tc % 2, lambda c, tc=tc: hT[:, c, tc * 512:(tc + 1) * 512], [B_hT[tc]])
                ld(0)
                ld(1)
                ld(2)
                na(0)
                for tc in range(8):
                    if tc + 1 < 8:
                        na(tc + 1)
                    nb(tc)
                    if tc + 3 < 8:
                        ld(tc + 3)
                sc.barrier()
            with ExitStack() as es1:
                es_cur[0] = es1
                eT = sb("eT", [8, S], F32)
                onesf = sb("onesf", [8, S], F32)
                cum = sb("cum", [8, S], F32)
                hi = sb("hi", [8, S], BF16)
                lo = sb("lo", [8, S], BF16)
                nhi = sb("nhi", [8, S], BF16)
                nlo = sb("nlo", [8, S], BF16)
                B_f = Buf()
                sc.op("pool", lambda e: e.memset(onesf[:], 1.0), writes=[B_f])
                for tc in range(8):
                    pi = tc % 2
                    for dc in range(8):
                        sc.op("pe", lambda e, dc=dc, tc=tc, pi=pi: e.matmul(
                            PS[pi][0:8, :], lhsT=wf[:, dc, :], rhs=hT[:, dc, tc * 512:(tc + 1) * 512],
                            start=(dc == 0), stop=(dc == 7)), reads=[B_wf, B_hT[tc]], writes=[PSB[pi]])
                    sc.op("act", lambda e, tc=tc, pi=pi: e.activation(out=eT[:, tc * 512:(tc + 1) * 512], in_=PS[pi][0:8, :],
                                                                     func=AF.Exp, scale=-1.0, bias=nbf[:, l:l + 1]),
                          reads=[PSB[pi], B_const], writes=[B_f])
                sc.op("act", lambda e: e.activation(out=eT[:], in_=eT[:], func=AF.Ln, bias=1.0), reads=[B_f], writes=[B_f])
                sc.op("dve", lambda e: e.tensor_tensor_scan(out=cum[:], data0=onesf[:], data1=eT[:], initial=0.0,
                                                            op0=ALU.mult, op1=ALU.subtract), reads=[B_f], writes=[B_f])
                sc.op("dve", lambda e: e.tensor_scalar(out=cum[:], in0=cum[:], scalar1=8.0, scalar2=None, op0=ALU.mult),
                      reads=[B_f], writes=[B_f])
                sc.op("dve", lambda e: e.tensor_copy(out=hi[:], in_=cum[:]), reads=[B_f], writes=[B_f])
                sc.op("dve", lambda e: e.tensor_tensor(out=eT[:], in0=cum[:], in1=hi[:], op=ALU.subtract), reads=[B_f], writes=[B_f])
                sc.op("dve", lambda e: e.tensor_copy(out=lo[:], in_=eT[:]), reads=[B_f], writes=[B_f])
                sc.op("dve", lambda e: e.tensor_scalar(out=nhi[:], in0=hi[:], scalar1=-1.0, scalar2=None, op0=ALU.mult),
                      reads=[B_f], writes=[B_f])
                sc.op("dve", lambda e: e.tensor_scalar(out=nlo[:], in0=lo[:], scalar1=-1.0, scalar2=None, op0=ALU.mult),
                      reads=[B_f], writes=[B_f])
                sc.dma("pool", QB[:, 64, :], hi[:], reads=[B_f], writes=B_QB)
                sc.dma("pool", QB[:, 65, :], lo[:], reads=[B_f], writes=B_QB)
                sc.dma("pool", KB[:, 66, :], nhi[:], reads=[B_f], writes=B_KB)
                sc.dma("pool", KB[:, 67, :], nlo[:], reads=[B_f], writes=B_KB)
                sc.barrier()
            with ExitStack() as es1:
                es_cur[0] = es1
                stq = [sb("stq%d" % i, [128, S], BF16) for i in range(2)]
                stb = [Buf() for _ in range(2)]
                vst = sb("vst", [128, 4, 32, 128], BF16)
                vsb = Buf()
                sc.op("pool", lambda e: e.memset(vst[:], 1.0), writes=[vsb])

                groups = []
                groups.append(("qk", 0, 512, [("QA", 0, 1), ("QA", 1, 1), ("KA", 0, 1), ("KA", 1, 1)]))
                groups.append(("v", 512, 256, (0, 4, 1)))
                groups.append(("qk", 768, 512, [("QB", b, 1) for b in range(4)]))
                groups.append(("qk", 1280, 512, [("KB", b, 1) for b in range(4)]))
                groups.append(("v", 1792, 256, (4, 4, 1)))
                groups.append(("v", 2048, 256, (8, 4, 1)))
                groups.append(("qk", 2312, 512, [("QC", b, DILS[b // 2]) for b in range(4)]))
                groups.append(("qk", 2824, 512, [("QC", 4, 16), ("QC", 5, 16), ("KC", 0, 1), ("KC", 1, 1)]))
                groups.append(("qk", 3336, 512, [("KC", b, DILS[b // 2]) for b in range(2, 6)]))
                for g in range(3):
                    groups.append(("v", 3848 + 256 * g, 256, (12 + 4 * g, 4, DILS[g])))

                def ldw(gi):
                    kind, c0, n, info = groups[gi]
                    sc.dma("pool", wbl[gi % 2][:, :, 0:n], wih[l, :, :, c0:c0 + n], writes=[wbb[gi % 2]])

                evac_i = [0]
                stq_i = [0]
                psrot = [0]
                for gi, (kind, c0, n, info) in enumerate(groups):
                    if gi + 1 < len(groups):
                        ldw(gi + 1)
                    w = wbl[gi % 2]
                    wb = wbb[gi % 2]
                    if kind == "qk":
                        for lb, (dst, b, dil) in enumerate(info):
                            si = stq_i[0] % 2
                            stq_i[0] += 1
                            st = stq[si]
                            for tc in range(8):
                                pi = psrot[0] % 4
                                psrot[0] += 1
                                for dc in range(8):
                                    sc.op("pe", lambda e, dc=dc, tc=tc, pi=pi, lb=lb: e.matmul(
                                        PS[pi][:, :], lhsT=w[:, dc, lb * 128:(lb + 1) * 128],
                                        rhs=hT[:, dc, tc * 512:(tc + 1) * 512], start=(dc == 0), stop=(dc == 7)),
                                        reads=[wb, B_hT[tc]], writes=[PSB[pi]])
                                if dil == 1:
                                    o_ap = st[:, tc * 512:(tc + 1) * 512]
                                    i_ap = PS[pi][:, :]
                                else:
                                    ni = 512 // dil
                                    o_ap = st[:, :].rearrange("p (r i) -> p r i", r=dil)[:, :, tc * ni:(tc + 1) * ni]
                                    i_ap = PS[pi][:, :].rearrange("p (j r) -> p r j", r=dil)
                                if evac_i[0] % 2 == 0:
                                    sc.op("act", act_copy(o_ap, i_ap), reads=[PSB[pi]], writes=[stb[si]])
                                else:
                                    sc.op("dve", lambda e, o_ap=o_ap, i_ap=i_ap: e.tensor_copy(out=o_ap, in_=i_ap),
                                          reads=[PSB[pi]], writes=[stb[si]])
                                evac_i[0] += 1
                            if dst == "QA":
                                sc.dma("sp", QA[b * 128:(b + 1) * 128, :], st[:], reads=[stb[si]], writes=[B_QA[b]])
                            elif dst == "KA":
                                sc.dma("sp", KA[b * 128:(b + 1) * 128, :], st[:], reads=[stb[si]], writes=[B_KA[b]])
                            elif dst == "QC":
                                sc.dma("sp", QC[b * 128:(b + 1) * 128, :], st[:], reads=[stb[si]], writes=[B_QC[b]])
                            elif dst == "KC":
                                sc.dma("sp", KC[b * 128:(b + 1) * 128, :], st[:], reads=[stb[si]], writes=[B_KC[b]])
                            else:
                                T_ = QB if dst == "QB" else KB
                                BB = B_QB if dst == "QB" else B_KB
                                for hh in range(2):
                                    sc.dma("sp", T_[2 * b + hh, 0:64, :], st[hh * 64:(hh + 1) * 64, :],
                                           reads=[stb[si]], writes=[BB[2 * b + hh]])
                    else:
                        h0, nh, dil = info
                        Lc = S // dil
                        for kb in range(32):
                            pi = psrot[0] % 4
                            psrot[0] += 1
                            r = (kb * 128) // Lc
                            i0 = (kb * 128) % Lc
                            t0 = i0 * dil + r
                            for dc in range(8):
                                sc.op("pe", lambda e, dc=dc, pi=pi, t0=t0, dil=dil, n=n: e.matmul(
                                    PS[pi][:, 0:n], lhsT=hT[:, dc, t0:t0 + 127 * dil + 1:dil], rhs=w[:, dc, 0:n],
                                    start=(dc == 0), stop=(dc == 7)), reads=[wb] + B_hT, writes=[PSB[pi]])
                            o_ap = vst[:, 0:nh, kb, 0:64]
                            i_ap = PS[pi][:, 0:n].rearrange("p (h e) -> p h e", e=64)
                            if evac_i[0] % 2 == 0:
                                sc.op("act", act_copy(o_ap, i_ap), reads=[PSB[pi]], writes=[vsb])
                            else:
                                sc.op("dve", lambda e, o_ap=o_ap, i_ap=i_ap: e.tensor_copy(out=o_ap, in_=i_ap),
                                      reads=[PSB[pi]], writes=[vsb])
                            evac_i[0] += 1
                        for hh in range(nh):
                            sc.dma("sp", VD[h0 + hh], vst[:, hh, :, :].rearrange("p k e -> p (k e)"),
                                   reads=[vsb], writes=[B_VD[h0 + hh]])
                sc.barrier()
        es_cur[0] = es

        with ExitStack() as es1:
            es_cur[0] = es1
            qTs = [sb("qT%d" % i, [128, S], BF16) for i in range(2)]
            kTs = [sb("kT%d" % i, [128, S], BF16) for i in range(2)]
            kT2s = [sb("kTb%d" % i, [128, S], BF16) for i in range(2)]
            vts = [sb("vt%d" % i, [128, 32, 128], BF16) for i in range(2)]
            ldq = [Buf() for _ in range(2)]
            ldk = [Buf() for _ in range(2)]
            ldk2 = [Buf() for _ in range(2)]
            ldv = [Buf() for _ in range(2)]
            for i_ in range(2):
                for t_, b_ in ((qTs[i_], ldq[i_]), (kTs[i_], ldk[i_]), (kT2s[i_], ldk2[i_])):
                    sc.op("dve", lambda e, t_=t_: e.memset(t_[:], 0.0), writes=[b_])
            pts = [sb("pt%d" % i, [128, 1024], BF16) for i in range(4)]
            ptb = [Buf() for _ in range(4)]
            rst = [sb("rst%d" % i, [64, 512], F32) for i in range(2)]
            rstb = [Buf() for _ in range(2)]
            on1 = sb("on1", [64, 512], F32)
            on2 = sb("on2", [64, 512], F32)
            on1b, on2b = Buf(), Buf()
            dds = [sb("dd%d" % i, [64, 512], F32) for i in range(2)]
            sqas = [sb("sqa%d" % i, [64, 512], BF16) for i in range(2)]
            rsa = sb("rsa", [64, 512], F32)
            ddbs = [Buf(), Buf()]
            sqabs = [Buf(), Buf()]
            rsab = Buf()
            mxs = [sb("mxs%d" % i, [64, S], BF16) for i in range(2)]
            mxb = [Buf() for _ in range(2)]
            mxC = sb("mxC", [64, S], BF16)
            mxCb = Buf()
            accC = sb("accC", [128, S], F32)
            accb = Buf()
            SBK = [0, 1, 2, 3]
            OBK = [4, 5, 6, 7]

            heads_ab = [("A", h) for h in range(4)] + [("B", h) for h in range(8)]
            heads_c = [("C", h, g) for h in range(4) for g in range(3)]
            heads = heads_ab + heads_c

            def load_head(hi_, hd):
                i = hi_ % 2
                if hd[0] == "A":
                    h = hd[1]
                    sc.dma("sp", qTs[i][0:64, :], QA[h * 64:(h + 1) * 64, :], reads=[B_QA[h // 2]], writes=[ldq[i]])
                    sc.dma("sp", kTs[i][0:32, :], KA[h * 64:h * 64 + 32, :], reads=[B_KA[h // 2]], writes=[ldk[i]])
                    sc.dma("sp", kT2s[i][32:64, :], KA[h * 64 + 32:h * 64 + 64, :], reads=[B_KA[h // 2]], writes=[ldk2[i]])
                    vi = h
                elif hd[0] == "B":
                    h = hd[1]
                    sc.dma("sp", qTs[i][0:68, :], QB[h], reads=[B_QB[h]], writes=[ldq[i]])
                    sc.dma("sp", kTs[i][0:68, :], KB[h], reads=[B_KB[h]], writes=[ldk[i]])
                    vi = 4 + h
                else:
                    h, g = hd[1], hd[2]
                    r0 = g * 256 + h * 64
                    if hi_ in (12, 13):
                        sc.op("dve", lambda e: e.memset(qTs[i][64:68, :], 0.0), writes=[ldq[i]])
                        sc.op("dve", lambda e: e.memset(kTs[i][64:68, :], 0.0), writes=[ldk[i]])
                    sc.dma("sp", qTs[i][0:64, :], QC[r0:r0 + 64, :], reads=[B_QC[r0 // 128]], writes=[ldq[i]])
                    sc.dma("sp", kTs[i][0:64, :], KC[r0:r0 + 64, :], reads=[B_KC[r0 // 128]], writes=[ldk[i]])
                    vi = 12 + g * 4 + h
                sc.dma("aq" if hd[0] == "C" else "sp", vts[i][:].rearrange("p k e -> p (k e)"), VD[vi],
                       reads=[B_VD[vi]], writes=[ldv[i]])

            cnt_pt = [0]
            cnt_s = [0]
            cnt_fin = [0]
            cnt_mx = [0]

            def normalize(obank, ncols, out_ap, out_buf):
                k = cnt_fin[0] % 2
                cnt_fin[0] += 1
                sc.op("dve", lambda e: e.reciprocal(out=rst[k][0:64, 0:ncols], in_=PS[obank][64:128, 0:ncols]),
                      reads=[PSB[obank]], writes=[rstb[k]])
                sc.op("dve", lambda e: e.tensor_tensor(out=out_ap, in0=PS[obank][0:64, 0:ncols], in1=rst[k][0:64, 0:ncols],
                                                       op=ALU.mult), reads=[PSB[obank], rstb[k]], writes=[out_buf])

            deferred = []

            def tick():
                for d_ in deferred:
                    d_[0] -= 1
                due = [d_ for d_ in deferred if d_[0] <= 0]
                for d_ in due:
                    deferred.remove(d_)
                    d_[1]()

            inflight_tiles = {}

            def busy_banks():
                b_ = set()
                for v_ in inflight_tiles.values():
                    b_ |= v_
                return b_

            POOLS = {"A": [0, 1, 2, 3], "B": [0, 1, 2, 3, 4, 5], "C": [0, 1, 2, 3]}

            def try_single(pool=None):
                pool = pool or POOLS["A"]
                bz = busy_banks()
                for _ in range(len(pool)):
                    b = pool[cnt_s[0] % len(pool)]
                    cnt_s[0] += 1
                    if b not in bz:
                        return b
                return None

            cur_typ = ["A"]

            def take_single():
                b = try_single(POOLS["B"] if cur_typ[0] == "B" else POOLS["A"])
                assert b is not None
                return b

            def try_pair(pool):
                bz = busy_banks()
                npair = len(pool) // 2
                st = (cnt_s[0] % len(pool)) // 2
                for k_ in range(npair):
                    p_ = 2 * ((st + k_) % npair)
                    if not ({p_, p_ + 1} & bz):
                        cnt_s[0] = p_ + 2
                        return p_
                return None

            def head_start_hook(hi_):
                if hi_ + 1 < len(heads):
                    load_head(hi_ + 1, heads[hi_ + 1])
                if 1 <= hi_ <= 8:
                    sc.dma("cv", W1b[hi_ - 1], w1h[l, hi_ - 1], writes=[B_W1b[hi_ - 1]])
                elif 9 <= hi_ <= 16:
                    sc.dma("cv", W2b[hi_ - 9], w2h[l, hi_ - 9], writes=[B_W2b[hi_ - 9]])

            def build_head(hi_, hd):
                i = hi_ % 2
                qT, kT, kT2, vt = qTs[i], kTs[i], kT2s[i], vts[i]
                lqk = [ldq[i], ldk[i], ldk2[i]]
                lv = [ldv[i]]
                typ = hd[0]
                tiles = []

                if typ in ("A", "B"):
                    h = hd[1]
                    ncomp = 2 if typ == "A" else 1
                    scale = SCALE_A if typ == "A" else SCALE_B
                    mi = cnt_mx[0] % 2
                    cnt_mx[0] += 1
                    mx, mxbuf = mxs[mi], mxb[mi]
                    for qc in range(8):
                        if typ == "A":
                            obanks = [OBK[(2 * qc + c) % 4] for c in range(ncomp)]
                        else:
                            obanks = [6 + (qc % 2)]
                        nkb = 4 * qc + 4
                        kb_start = 0
                        if typ == "B":
                            kb_start = 4 * qc
                            for kb2 in range(0, 4 * qc, 2):
                                def qk2(sbk, kb2=kb2, qc=qc):
                                    for u_ in range(2):
                                        sc.op("pe", lambda e: e.matmul(PS[sbk + u_][:, 0:512], lhsT=kT[:, (kb2 + u_) * 128:(kb2 + u_ + 1) * 128],
                                                                       rhs=qT[:, qc * 512:(qc + 1) * 512], start=True, stop=True),
                                              reads=lqk, writes=[PSB[sbk + u_]])

                                def ex2(sbk, pk):
                                    sc.op("act", lambda e: e.activation(out=pts[pk][:, 0:1024], in_=PSall[:, sbk * 512:(sbk + 2) * 512],
                                                                        func=AF.Exp, scale=scale),
                                          reads=[PSB[sbk], PSB[sbk + 1]], writes=[ptb[pk]])

                                def av2(pk, kb2=kb2, nkb=nkb, obanks=obanks):
                                    ob = obanks[0]
                                    for u_ in range(2):
                                        sc.op("pe", lambda e: e.matmul(PS[ob][:, 0:512], lhsT=vt[:, kb2 + u_, :], rhs=pts[pk][:, u_ * 512:(u_ + 1) * 512],
                                                                       start=(kb2 + u_ == 0), stop=False),
                                              reads=[ptb[pk]] + lv, writes=[PSB[ob]])
                                tiles.append((qk2, ex2, av2, None, "pair"))
                        for kb in range(kb_start, nkb):
                            j = kb - 4 * qc
                            col0 = max(0, j) * 128
                            ncol = 512 - col0
                            q0 = qc * 512 + col0
                            for c in range(ncomp):
                                def qk(sbk, kb=kb, col0=col0, ncol=ncol, q0=q0, c=c, j=j):
                                    if typ == "A":
                                        off = q0 - kb * 128
                                        far = (off - 127 >= DTHR)
                                        sc.op("pe", lambda e: e.matmul(PS[sbk][:, col0:512], lhsT=(kT if c == 0 else kT2)[:, kb * 128:(kb + 1) * 128],
                                                                       rhs=qT[:, q0:q0 + ncol], start=True, stop=far),
                                              reads=lqk, writes=[PSB[sbk]])
                                        if not far:
                                            sc.op("pe", lambda e: e.matmul(PS[sbk][:, col0:512], lhsT=antid[:], rhs=tshA[:, h, off:off + ncol],
                                                                           start=False, stop=True), reads=[B_const], writes=[PSB[sbk]])
                                    else:
                                        if j >= 0:
                                            sc.op("pe", lambda e: e.matmul(PS[sbk][:, col0:col0 + 128], lhsT=kT[:, kb * 128:(kb + 1) * 128],
                                                                           rhs=qT[:, q0:q0 + 128], start=True, stop=False),
                                                  reads=lqk, writes=[PSB[sbk]])
                                            sc.op("pe", lambda e: e.matmul(PS[sbk][:, col0:col0 + 128], lhsT=ident[:], rhs=maskb[:],
                                                                           start=False, stop=True), reads=[B_const], writes=[PSB[sbk]])
                                            if ncol > 128:
                                                sc.op("pe", lambda e: e.matmul(PS[sbk][:, col0 + 128:512], lhsT=kT[:, kb * 128:(kb + 1) * 128],
                                                                               rhs=qT[:, q0 + 128:q0 + ncol], start=True, stop=True),
                                                      reads=lqk, writes=[PSB[sbk]])
                                        else:
                                            sc.op("pe", lambda e: e.matmul(PS[sbk][:, 0:512], lhsT=kT[:, kb * 128:(kb + 1) * 128],
                                                                           rhs=qT[:, q0:q0 + 512], start=True, stop=True),
                                                  reads=lqk, writes=[PSB[sbk]])

                                def ex(sbk, pk, col0=col0, far=(typ == "A" and (q0 - kb * 128 - 127 >= DTHR))):
                                    if far:
                                        sc.op("act", lambda e: e.activation(out=pts[pk][:, col0:512], in_=PS[sbk][:, col0:512],
                                                                            func=AF.Exp, scale=scale, bias=rb31[:, 0, h:h + 1]),
                                              reads=[PSB[sbk], B_const], writes=[ptb[pk]])
                                    else:
                                        sc.op("act", lambda e: e.activation(out=pts[pk][:, col0:512], in_=PS[sbk][:, col0:512],
                                                                            func=AF.Exp, scale=scale),
                                              reads=[PSB[sbk]], writes=[ptb[pk]])

                                def av(pk, kb=kb, col0=col0, c=c, nkb=nkb, obanks=obanks):
                                    ob = obanks[c]
                                    sc.op("pe", lambda e: e.matmul(PS[ob][:, col0:512], lhsT=vt[:, kb, :], rhs=pts[pk][:, col0:512],
                                                                   start=(kb == 0), stop=(kb == nkb - 1)),
                                          reads=[ptb[pk]] + lv, writes=[PSB[ob]])

                                post = None
                                if kb == nkb - 1 and c == ncomp - 1:
                                    def post(qc=qc, obanks=obanks):
                                        cs = slice(qc * 512, (qc + 1) * 512)
                                        if typ == "B":
                                            normalize(obanks[0], 512, mx[0:64, cs], mxbuf)
                                        else:
                                            dd, sqa, ddb, sqab = dds[qc % 2], sqas[qc % 2], ddbs[qc % 2], sqabs[qc % 2]
                                            normalize(obanks[0], 512, on1[:], on1b)
                                            normalize(obanks[1], 512, on2[:], on2b)
                                            sc.op("dve", lambda e: e.scalar_tensor_tensor(
                                                out=dd[:], in0=on2[:], scalar=neglam[0:64, l:l + 1], in1=on1[:],
                                                op0=ALU.mult, op1=ALU.add), reads=[on1b, on2b, B_const], writes=[ddb])
                                            sc.op("dve", lambda e: e.tensor_tensor(out=sqa[:], in0=dd[:], in1=dd[:], op=ALU.mult), reads=[ddb], writes=[sqab])

                                            def post2(cs=cs, dd=dd, sqa=sqa, ddb=ddb, sqab=sqab, mx=mx, mxbuf=mxbuf):
                                                MBK = take_single()
                                                sc.op("pe", lambda e: e.matmul(PS[MBK][0:64, :], lhsT=ones_bf[0:64, 0:64], rhs=sqa[:],
                                                                               start=True, stop=True), reads=[sqab, B_const], writes=[PSB[MBK]])
                                                sc.op("act", lambda e: e.activation(out=rsa[:], in_=PS[MBK][0:64, :], func=AF.Ln, scale=1.0 / 64,
                                                                                    bias=epsc[0:64, 0:1]), reads=[PSB[MBK], B_const], writes=[rsab])
                                                sc.op("act", lambda e: e.activation(out=rsa[:], in_=rsa[:], func=AF.Exp, scale=-0.5),
                                                      reads=[rsab], writes=[rsab])
                                                sc.op("dve", lambda e: e.scalar_tensor_tensor(
                                                    out=mx[0:64, cs], in0=dd[:], scalar=dngs[0:64, l:l + 1], in1=rsa[:],
                                                    op0=ALU.mult, op1=ALU.mult), reads=[ddb, rsab, B_const], writes=[mxbuf])
                                            deferred.append([20, post2])
                                tiles.append((qk, ex, av, post, "single"))
                    row0 = (h * 64) if typ == "A" else (256 + h * 64)

                    def head_done(mx=mx, mxbuf=mxbuf, row0=row0):
                        sc.dma("pool", MXD[row0 // 128, row0 % 128:row0 % 128 + 64, :], mx[:], reads=[mxbuf], writes=[B_MX[row0 // 128]])
                else:
                    h, g = hd[1], hd[2]
                    dil = DILS[g]
                    Lc = S // dil
                    gh = g * 4 + h
                    for qc in range(8):
                        ob = OBK[qc % 4]
                        for half in range(2):
                            qb0 = qc * 4 + half * 2
                            hp0 = (qb0 * 128) % Lc != 0
                            cst = 0 if hp0 else 128

                            def qk(sbk, qb0=qb0, hp0=hp0):
                                for u_ in range(2):
                                    qb = qb0 + u_
                                    cb = 256 * u_
                                    if u_ == 1 or hp0:
                                        sc.op("pe", lambda e: e.matmul(PS[sbk][:, cb:cb + 128], lhsT=kT[:, (qb - 1) * 128:qb * 128],
                                                                       rhs=qT[:, qb * 128:(qb + 1) * 128], start=True, stop=False),
                                              reads=lqk, writes=[PSB[sbk]])
                                        sc.op("pe", lambda e: e.matmul(PS[sbk][:, cb:cb + 128], lhsT=antid[:], rhs=tshC[:, gh, 128:256],
                                                                       start=False, stop=True), reads=[B_const], writes=[PSB[sbk]])
                                    sc.op("pe", lambda e: e.matmul(PS[sbk][:, cb + 128:cb + 256], lhsT=kT[:, qb * 128:(qb + 1) * 128],
                                                                   rhs=qT[:, qb * 128:(qb + 1) * 128], start=True, stop=False),
                                          reads=lqk, writes=[PSB[sbk]])
                                    sc.op("pe", lambda e: e.matmul(PS[sbk][:, cb + 128:cb + 256], lhsT=antid[:], rhs=tshC[:, gh, 0:128],
                                                                   start=False, stop=True), reads=[B_const], writes=[PSB[sbk]])

                            def ex(sbk, pk, cst=cst):
                                sc.op("act", lambda e: e.activation(out=pts[pk][:, cst:512], in_=PS[sbk][:, cst:512],
                                                                    func=AF.Exp, scale=SCALE_B),
                                      reads=[PSB[sbk]], writes=[ptb[pk]])

                            def av(pk, qb0=qb0, half=half, hp0=hp0, ob=ob):
                                for u_ in range(2):
                                    qb = qb0 + u_
                                    cb = 256 * u_
                                    oc = slice((half * 2 + u_) * 128, (half * 2 + u_ + 1) * 128)
                                    hp = (u_ == 1 or hp0)
                                    if hp:
                                        sc.op("pe", lambda e: e.matmul(PS[ob][:, oc], lhsT=vt[:, qb - 1, :], rhs=pts[pk][:, cb:cb + 128],
                                                                       start=True, stop=False), reads=[ptb[pk]] + lv, writes=[PSB[ob]])
                                    sc.op("pe", lambda e: e.matmul(PS[ob][:, oc], lhsT=vt[:, qb, :], rhs=pts[pk][:, cb + 128:cb + 256],
                                                                   start=(not hp), stop=True), reads=[ptb[pk]] + lv, writes=[PSB[ob]])

                            post = None
                            if half == 1:
                                def post(qc=qc, ob=ob):
                                    if dil == 1:
                                        sc.op("act", act_copy(accC[:, qc * 512:(qc + 1) * 512], PS[ob][:, :]),
                                              reads=[PSB[ob]], writes=[accb])
                                    else:
                                        nr = max(1, 512 // Lc)
                                        ni = 512 // nr
                                        r0 = (qc * 512) // Lc
                                        i0 = (qc * 512) % Lc
                                        av_ = accC[:, :].rearrange("p (i r) -> p r i", r=dil)[:, r0:r0 + nr, i0:i0 + ni]
                                        pv = PS[ob][:, :].rearrange("p (r i) -> p r i", r=nr)
                                        sc.op("dve", lambda e: e.tensor_tensor(out=av_, in0=av_, in1=pv, op=ALU.add),
                                              reads=[PSB[ob], accb], writes=[accb])
                            tiles.append((qk, ex, av, post, "single"))

                    def head_done(h=h, g=g):
                        if g != 2:
                            return
                        mx, mxbuf = mxC, mxCb
                        for qc in range(8):
                            cs = slice(qc * 512, (qc + 1) * 512)
                            k = cnt_fin[0] % 2
                            cnt_fin[0] += 1
                            sc.op("act", lambda e: e.activation(out=rst[k][0:64, :], in_=accC[64:128, cs], func=AF.Ln),
                                  reads=[accb], writes=[rstb[k]])
                            sc.op("act", lambda e: e.activation(out=rst[k][0:64, :], in_=rst[k][0:64, :], func=AF.Exp, scale=-1.0),
                                  reads=[rstb[k]], writes=[rstb[k]])
                            sc.op("dve", lambda e: e.tensor_tensor(out=mx[0:64, cs], in0=accC[0:64, cs], in1=rst[k][0:64, :],
                                                                   op=ALU.mult), reads=[accb, rstb[k]], writes=[mxbuf])
                        row0 = 768 + h * 64
                        sc.dma("pool", MXD[row0 // 128, row0 % 128:row0 % 128 + 64, :], mx[:], reads=[mxbuf], writes=[B_MX[row0 // 128]])

                return tiles, head_done, typ

            seq = []
            for hi_, hd in enumerate(heads):
                tiles_, hdone_, typ_ = build_head(hi_, hd)
                for ti_, tl_ in enumerate(tiles_):
                    sh_ = (lambda hi_=hi_: head_start_hook(hi_)) if ti_ == 0 else None
                    eh_ = None
                    if ti_ == len(tiles_) - 1:
                        if typ_ == "A":
                            eh_ = (lambda hdone_=hdone_: deferred.append([22, hdone_]))
                        else:
                            eh_ = hdone_
                    seq.append(tl_[0:4] + (sh_, eh_, tl_[4], typ_))
            load_head(0, heads[0])
            LA = 2
            n = len(seq)
            assign = {}
            def consume(u):
                pk = cnt_pt[0] % 4
                cnt_pt[0] += 1
                seq[u][1](assign[u], pk)
                seq[u][2](pk)
                inflight_tiles.pop(u, None)
                if seq[u][3] is not None:
                    seq[u][3]()
                if seq[u][4] is not None:
                    seq[u][4]()
                if seq[u][5] is not None:
                    seq[u][5]()
                tick()

            def try_issue(t):
                pool = POOLS[seq[t][7]]
                if seq[t][6] == "pair":
                    sbk = try_pair(pool)
                    banks = None if sbk is None else {sbk, sbk + 1}
                else:
                    sbk = try_single(pool)
                    banks = None if sbk is None else {sbk}
                if sbk is None:
                    return False
                inflight_tiles[t] = banks
                cur_typ[0] = seq[t][7]
                assign[t] = sbk
                seq[t][0](sbk)
                return True

            for t in range(n + LA):
                u = t - LA
                issued = (t < n) and try_issue(t)
                if u >= 0:
                    consume(u)
                if t < n and not issued:
                    ok_ = try_issue(t)
                    assert ok_
            while deferred:
                deferred.pop(0)[1]()
            sc.barrier()
        es_cur[0] = es

        with ExitStack() as es1:
            es_cur[0] = es1
            wo = sb("wo", [128, 8, D], BF16)
            wob = Buf()
            sc.dma("pool", wo[:], woh[l], writes=[wob])
            mxc = [sb("mxc%d" % i, [128, 8, 512], BF16) for i in range(2)]
            mxcb = [Buf() for _ in range(2)]
            xcs = [sb("xf%d" % i, [128, 8, 512], F32) for i in range(2)]
            xcb = [Buf() for _ in range(2)]
            sq = sb("sqf", [128, 8, 512], BF16)
            sqb_ = Buf()
            rs2 = sb("rs2", [128, 512], F32)
            rs2b = Buf()
            h2s = [sb("h2_%d" % i, [128, 8, 512], BF16) for i in range(2)]
            h2bs = [Buf() for _ in range(2)]
            uT = sb("uT", [128, 32, 512], BF16)
            uTb = [Buf() for _ in range(32)]
            rl = [sb("rl%d" % i, [128, 512], F32) for i in range(2)]
            rlb = [Buf() for _ in range(2)]
            w1s = [sb("w1s%d" % i, [128, 8, 512], BF16) for i in range(2)]
            w1sb = [Buf() for _ in range(2)]
            w2s = [sb("w2s%d" % i, [128, 32, 128], BF16) for i in range(2)]
            w2sb = [Buf() for _ in range(2)]
            last = (l == n_layers - 1)

            def ldx(tc):
                i = tc % 2
                sc.dma("sp", xcs[i][:], xsrc[:, :, tc * 512:(tc + 1) * 512].rearrange("c p t -> p c t"),
                       reads=[B_xr[tc]], writes=[xcb[i]])
                sc.dma("sp", mxc[i][:], MXD[:, :, tc * 512:(tc + 1) * 512].rearrange("c p t -> p c t"),
                       reads=B_MX, writes=[mxcb[i]])

            wiss = [0, 0]

            def ensure_w1(n):
                while wiss[0] <= min(n, 63):
                    m = wiss[0]
                    sc.dma("sp", w1s[m % 2][:].rearrange("p a b -> p (a b)"), W1b[m % 8], reads=[B_W1b[m % 8]], writes=[w1sb[m % 2]])
                    wiss[0] += 1

            def ensure_w2(n):
                while wiss[1] <= min(n, 63):
                    m = wiss[1]
                    sc.dma("sp", w2s[m % 2][:].rearrange("p a b -> p (a b)"), W2b[m % 8], reads=[B_W2b[m % 8]], writes=[w2sb[m % 2]])
                    wiss[1] += 1

            prot = [0]

            def pre(tc):
                i = tc % 2
                xc, xb = xcs[i], xcb[i]
                for dc in range(8):
                    pi = prot[0] % 4
                    prot[0] += 1
                    for fc in range(8):
                        sc.op("pe", lambda e, dc=dc, fc=fc, pi=pi: e.matmul(
                            PS[pi][:, :], lhsT=wo[:, fc, dc * 128:(dc + 1) * 128], rhs=mxc[i][:, fc, :],
                            start=(fc == 0), stop=(fc == 7)), reads=[wob, mxcb[i]], writes=[PSB[pi]])
                    sc.op("dve", lambda e, dc=dc, pi=pi: e.tensor_tensor(out=xc[:, dc, :], in0=xc[:, dc, :], in1=PS[pi][:, :], op=ALU.add),
                          reads=[PSB[pi], xb], writes=[xb])
                norm_a(xc, xb, sq, sqb_)

            def pre_b(tc):
                i = tc % 2
                xc, xb = xcs[i], xcb[i]
                norm_b(xc, xb, lambda c: g2s[:, l * 8 + c:l * 8 + c + 1], rs2, rs2b, sq, sqb_, 4,
                       lambda c: h2s[i][:, c, :], [h2bs[i]])

            def up(tc):
                h2, h2b = h2s[tc % 2], h2bs[tc % 2]
                for fb in range(8):
                    n = tc * 8 + fb
                    ensure_w1(n + 1)
                    if fb == 6:
                        ensure_w2(tc * 8)
                    if fb == 2 and tc >= 1 and tc + 1 < 8:
                        ldx(tc + 1)
                    cur = n % 2
                    for fl in range(4):
                        fc = fb * 4 + fl
                        pi = prot[0] % 4
                        prot[0] += 1
                        for dc in range(8):
                            sc.op("pe", lambda e, dc=dc, fl=fl, pi=pi, cur=cur: e.matmul(
                                PS[pi][:, :], lhsT=w1s[cur][:, dc, fl * 128:(fl + 1) * 128], rhs=h2[:, dc, :],
                                start=(dc == 0), stop=(dc == 7)), reads=[w1sb[cur], h2b], writes=[PSB[pi]])
                        k = fc % 2
                        sc.op("act", lambda e, pi=pi, k=k: e.activation(out=rl[k][:], in_=PS[pi][:, :], func=AF.Relu),
                              reads=[PSB[pi]], writes=[rlb[k]])
                        eng = "pool" if fc % 2 == 0 else "dve"
                        sc.op(eng, lambda e, fc=fc, k=k: e.tensor_tensor(out=uT[:, fc, :], in0=rl[k][:], in1=rl[k][:], op=ALU.mult),
                              reads=[rlb[k]], writes=[uTb[fc]])

            def down(tc, hook=None):
                i = tc % 2
                xc, xb = xcs[i], xcb[i]
                for dc in range(8):
                    if dc == 2 and hook is not None:
                        hook()
                    n = tc * 8 + dc
                    ensure_w2(n + 1)
                    cur = n % 2
                    pi = prot[0] % 4
                    prot[0] += 1
                    for fc in range(32):
                        sc.op("pe", lambda e, fc=fc, pi=pi, cur=cur: e.matmul(
                            PS[pi][:, :], lhsT=w2s[cur][:, fc, :], rhs=uT[:, fc, :],
                            start=(fc == 0), stop=(fc == 31)), reads=[w2sb[cur], uTb[fc]], writes=[PSB[pi]])
                    sc.op("dve", lambda e, dc=dc, pi=pi: e.tensor_tensor(out=xc[:, dc, :], in0=xc[:, dc, :], in1=PS[pi][:, :], op=ALU.add),
                          reads=[PSB[pi], xb], writes=[xb])
                if not last:
                    sc.dma("pool", xr[:, :, tc * 512:(tc + 1) * 512].rearrange("c p t -> p c t"), xc[:], reads=[xb], writes=[B_xr[tc]])
                else:
                    norm_chunk(xc, xb, lambda c: gfs[:, c:c + 1], rs2, rs2b, sq, sqb_, 4,
                               lambda c: xc[:, c, :], [xb])
                    sc.dma("pool", outT[:, :, tc * 512:(tc + 1) * 512].rearrange("c p t -> p c t"), xc[:], reads=[xb], writes=[B_xr[tc]])

            ldx(0)
            ldx(1)
            ensure_w1(0)
            pre(0)
            pre_b(0)
            for tc in range(8):
                up(tc)
                if tc + 1 < 8:
                    pre(tc + 1)
                    down(tc, hook=lambda tc=tc: pre_b(tc + 1))
                else:
                    down(tc)
            sc.barrier(include_cv=True)
        es_cur[0] = es

    sc.barrier(include_cv=True)
    es.close()
    return nc


def _t5_bucket(d):
    d = np.maximum(d, 0)
    df = np.maximum(d, 1).astype(np.float32)
    large = 16 + (np.log(df / np.float32(16)) / np.float32(math.log(2048 / 16)) * np.float32(16)).astype(np.int32)
    large = np.minimum(large, 31)
    return np.where(d < 16, d, large)


def _consts():
    oha = np.zeros((33, TA_LEN), np.float32)
    i = np.arange(TA_LEN)
    dist = i - 127
    bk = _t5_bucket(dist)
    ok = dist >= 0
    oha[bk[ok], i[ok]] = 1.0
    oha[32, i[~ok]] = 1.0
    ohc = np.zeros((33, 3 * TC_LEN), np.float32)
    for g, dil in enumerate(DILS):
        i = np.arange(TC_LEN)
        jj = i - 127
        ok = (jj >= 0) & (jj <= 128)
        bk = _t5_bucket(jj * dil)
        ohc[bk[ok], g * TC_LEN + i[ok]] = 1.0
        ohc[32, g * TC_LEN + i[~ok]] = 1.0
    return oha, ohc


DTHR = int(np.nonzero(_t5_bucket(np.arange(4096)) == 31)[0][0])
_NC_CACHE = {}


def _prep_shared(inp):
    f = np.float32
    w_in = np.asarray(inp["w_in"], f)
    w_o = np.asarray(inp["w_o"], f)
    w_1 = np.asarray(inp["w_1"], f)
    w_2 = np.asarray(inp["w_2"], f)
    sh = {}
    sh["wih"] = np.ascontiguousarray(w_in.reshape(L, 8, 128, NIN).transpose(0, 2, 1, 3))
    sh["woh"] = np.ascontiguousarray(w_o.reshape(L, 8, 128, D).transpose(0, 2, 1, 3))
    sh["w1h"] = np.ascontiguousarray(w_1.reshape(L, 8, 128, 8, 512).transpose(0, 3, 2, 1, 4)).reshape(L, 8, 128, 8 * 512)
    sh["w2h"] = np.ascontiguousarray(w_2.reshape(L, 32, 128, 8, 128).transpose(0, 3, 2, 1, 4)).reshape(L, 8, 128, 32 * 128)
    sh["g1T"] = np.ascontiguousarray(np.asarray(inp["norm1_g"], f).reshape(L, 8, 128).transpose(2, 0, 1)).reshape(128, L * 8)
    sh["g2T"] = np.ascontiguousarray(np.asarray(inp["norm2_g"], f).reshape(L, 8, 128).transpose(2, 0, 1)).reshape(128, L * 8)
    sh["gfT"] = np.ascontiguousarray(np.asarray(inp["final_g"], f).reshape(8, 128).T)
    sh["bfT"] = np.ascontiguousarray(np.asarray(inp["b_f"], f).T)
    lam = np.stack([np.asarray(inp[k], f) for k in ("lam_q1", "lam_k1", "lam_q2", "lam_k2")], axis=1)
    sh["lamv"] = np.ascontiguousarray(lam.reshape(1, L * 128))
    sh["dng"] = np.ascontiguousarray(np.asarray(inp["diff_norm_g"], f).T)
    rbm = np.full((33, 16), NEG, f)
    rbm[:32] = np.asarray(inp["rel_bias"], f)
    sh["rbm"] = rbm
    oha, ohc = _consts()
    sh["oha"] = oha
    sh["ohc"] = ohc
    return sh


def kernel(**inputs):
    x = np.asarray(inputs["x"], np.float32)
    nb = x.shape[0]
    sh = _prep_shared(inputs)
    if "nc" not in _NC_CACHE:
        _NC_CACHE["nc"] = build_nc()
    nc = _NC_CACHE["nc"]
    in_maps = []
    for b in range(nb):
        m = dict(sh)
        m["xT"] = np.ascontiguousarray(x[b].T).reshape(8, 128, S)
        in_maps.append(m)
    res = run_bass_kernel_spmd(nc, in_maps, core_ids=list(range(nb)))
    out = np.empty((nb, S, D), np.float32)
    for b in range(nb):
        out[b] = res.results[b]["outT"].reshape(D, S).T
    return out
```
